# Optimizing a Trainium2 kernel written in Bass

```python
import jax, jax.numpy as jnp
from jax import lax
import numpy as np

D_MODEL = 2048
BATCH = 8
SEQ = 2048
DEPTH = 1

CHUNK = 64
EPS = 1e-6

POOL_WINDOWS = (2, 4, 8, 16)
POOL_GROUPS = len(POOL_WINDOWS)
POOL_WIDTH = D_MODEL // 2
POOL_GROUP = POOL_WIDTH // POOL_GROUPS
POOL_OUT_GROUP = D_MODEL // POOL_GROUPS

GLA_HEADS = 4
GLA_DK = D_MODEL // 2 // GLA_HEADS
GLA_DV = D_MODEL // GLA_HEADS
QK_WIDTH = GLA_HEADS * GLA_DK
V_WIDTH = GLA_HEADS * GLA_DV
GATE_RANK = 16
GATE_TAU = 16.0

N_BRANCHES = 2
GATE_WIDTH = N_BRANCHES * D_MODEL

D_FF = 4 * D_MODEL

SPLITS = (
    POOL_WIDTH,
    POOL_WIDTH + QK_WIDTH,
    POOL_WIDTH + 2 * QK_WIDTH,
    POOL_WIDTH + 2 * QK_WIDTH + V_WIDTH,
    POOL_WIDTH + 2 * QK_WIDTH + 2 * V_WIDTH,
    POOL_WIDTH + 2 * QK_WIDTH + 2 * V_WIDTH + GATE_RANK,
)
IN_WIDTH = POOL_WIDTH + 2 * QK_WIDTH + 2 * V_WIDTH + GATE_RANK + GATE_WIDTH

kernel_name = "hybrid_pool_gla_gated_block"


def rmsnorm(x, g):
    xf = x.astype(jnp.float32)
    y = xf * lax.rsqrt(jnp.mean(xf * xf, axis=-1, keepdims=True) + EPS)
    return (y * g.astype(jnp.float32)).astype(x.dtype)


def pool_mixer(u, w_groups, scale):
    b, s, _ = u.shape
    uf = u.astype(jnp.float32)
    csum = jnp.cumsum(uf, axis=1)
    count = jnp.arange(1, s + 1, dtype=jnp.float32)[None, :, None]
    diffs = []
    for gi, w in enumerate(POOL_WINDOWS):
        sl = slice(gi * POOL_GROUP, (gi + 1) * POOL_GROUP)
        cg = csum[..., sl]
        c_prev = jnp.pad(cg, ((0, 0), (w, 0), (0, 0)))[:, :s]
        mean = (cg - c_prev) / jnp.minimum(count, float(w))
        diffs.append(mean - uf[..., sl])
    d = jnp.stack(diffs, axis=2).astype(u.dtype)
    y = jnp.einsum('bsgc,gce->bsge', d, w_groups).reshape(b, s, D_MODEL)
    return y * scale


def gla_mixer(q, k, v, g, a_low, w_alpha, b_alpha, norm_g):
    b, s, _ = q.shape
    nc = s // CHUNK
    log_a = jax.nn.log_sigmoid((a_low @ w_alpha + b_alpha).astype(jnp.float32)) / GATE_TAU

    def chunked(t, d):
        return t.astype(jnp.float32).reshape(b, nc, CHUNK, GLA_HEADS, d).transpose(1, 0, 3, 2, 4)

    qc = chunked(q, GLA_DK) * (GLA_DK ** -0.5)
    kc = chunked(k, GLA_DK)
    vc = chunked(v, GLA_DV)
    cum = jnp.cumsum(chunked(log_a, GLA_DK), axis=3)
    last = cum[:, :, :, -1:, :]
    k_dec = kc * jnp.exp(last - cum)
    chunk_decay = jnp.exp(last[:, :, :, 0, :])

    def step(state, inp):
        q_c, k_c, v_c, a_c = inp
        state = a_c[..., None] * state + jnp.einsum('bhcd,bhce->bhde', k_c, v_c)
        return state, jnp.einsum('bhcd,bhde->bhce', q_c, state)

    s0 = jnp.zeros((b, GLA_HEADS, GLA_DK, GLA_DV), jnp.float32)
    _, o = lax.scan(step, s0, (qc, k_dec, vc, chunk_decay))
    o = o.transpose(1, 0, 3, 2, 4).reshape(b, s, GLA_HEADS, GLA_DV)
    o = o * lax.rsqrt(jnp.mean(o * o, axis=-1, keepdims=True) + EPS) * norm_g.astype(jnp.float32)
    o = o.reshape(b, s, V_WIDTH) * jax.nn.silu(g.astype(jnp.float32))
    return o.astype(q.dtype)


def setup_inputs(seed: int = 0) -> dict:
    key = jax.random.key(seed)
    ks = jax.random.split(key, 16)
    f32 = jnp.float32

    def nrm(k, shape, scale):
        return jax.random.normal(k, shape, f32) * scale

    L = DEPTH
    return {
        "x": jax.random.normal(ks[0], (BATCH, SEQ, D_MODEL), f32),
        "norm_mix_g": 1.0 + nrm(ks[1], (L, D_MODEL), 0.02),
        "w_in": nrm(ks[2], (L, D_MODEL, IN_WIDTH), D_MODEL ** -0.5),
        "pool_w": nrm(ks[3], (L, POOL_GROUPS, POOL_GROUP, POOL_OUT_GROUP), POOL_GROUP ** -0.5),
        "pool_scale": 1.0 + nrm(ks[4], (L, D_MODEL), 0.02),
        "w_alpha": nrm(ks[5], (L, GATE_RANK, QK_WIDTH), GATE_RANK ** -0.5),
        "b_alpha": nrm(ks[6], (L, QK_WIDTH), 0.02),
        "gla_norm_g": 1.0 + nrm(ks[7], (L, GLA_HEADS, GLA_DV), 0.02),
        "w_gla_out": nrm(ks[8], (L, V_WIDTH, D_MODEL), V_WIDTH ** -0.5),
        "w_out": nrm(ks[9], (L, D_MODEL, D_MODEL), D_MODEL ** -0.5),
        "norm_mlp_g": 1.0 + nrm(ks[10], (L, D_MODEL), 0.02),
        "w_mlp_up": nrm(ks[11], (L, D_MODEL, D_FF), D_MODEL ** -0.5),
        "w_mlp_down": nrm(ks[12], (L, D_FF, D_MODEL), D_FF ** -0.5),
        "norm_final_g": 1.0 + nrm(ks[13], (D_MODEL,), 0.02),
    }


def reference(x, norm_mix_g, w_in, pool_w, pool_scale, w_alpha, b_alpha, gla_norm_g,
              w_gla_out, w_out, norm_mlp_g, w_mlp_up, w_mlp_down, norm_final_g):
    for l in range(DEPTH):
        h = rmsnorm(x, norm_mix_g[l])
        proj = h @ w_in[l]
        u, q, k, v, g, a_low, gate_logits = jnp.split(proj, SPLITS, axis=-1)
        y_pool = pool_mixer(u, pool_w[l], pool_scale[l])
        y_gla = gla_mixer(q, k, v, g, a_low, w_alpha[l], b_alpha[l], gla_norm_g[l]) @ w_gla_out[l]
        gate_pool, gate_gla = jnp.split(jax.nn.sigmoid(gate_logits), N_BRANCHES, axis=-1)
        mixed = gate_pool * y_pool + gate_gla * y_gla
        x = x + mixed @ w_out[l]
        h = rmsnorm(x, norm_mlp_g[l])
        x = x + jnp.square(jax.nn.relu(h @ w_mlp_up[l])) @ w_mlp_down[l]
    return rmsnorm(x, norm_final_g)
```

```python
import contextlib
import numpy as np
import concourse.bass as bass
import concourse.mybir as mybir
from concourse.bass_utils import run_bass_kernel_spmd

F32 = mybir.dt.float32
BF16 = mybir.dt.bfloat16
F32R = mybir.dt.float32r
AF = mybir.ActivationFunctionType
ALU = mybir.AluOpType

ENGS = ("pe", "act", "dve", "pool", "sp")

D = 2048
S_LEN = 2048
TH = 1024
NH = S_LEN // TH
KC = 16
EPS = 1e-6
IN_W = 11280
U0, Q0, K0, V0, G0, AL0, GP0, GG0 = 0, 1024, 2048, 3072, 5120, 7168, 7184, 9232
DFF = 8192
POOL_WINDOWS = (2, 4, 8, 16)

VG1, VPS, VGN, VG2, VGF = 0, 16, 32, 48, 64
C_ONES, C_TRI, C_IND, C_INV, C_ID = 0, 128, 256, 258, 322
C_W = 450


class Res:
    __slots__ = ("name", "last_w", "readers")

    def __init__(self, name):
        self.name = name
        self.last_w = None
        self.readers = {}


def _add_reader(res, ev):
    key = (ev[0], ev[1])
    if ev[2] > res.readers.get(key, -1):
        res.readers[key] = ev[2]


class Op:
    __slots__ = ("eng", "fn", "waits", "signal", "dma_sem", "dma_cnt")


class Prog:
    def __init__(self, nc):
        self.nc = nc
        self.ops = {e: [] for e in ENGS}
        self.dma_counts = {}
        self.n_dma_sems = 0
        self.out_events = []

    def new_dma_sem(self):
        i = self.n_dma_sems
        self.n_dma_sems += 1
        self.dma_counts[i] = 0
        return i

    @staticmethod
    def _deps(reads, writes):
        deps = {}

        def add(ev):
            key = (ev[0], ev[1])
            if ev[2] > deps.get(key, -1):
                deps[key] = ev[2]

        for r in reads:
            if r.last_w is not None:
                add(r.last_w)
        for w in writes:
            if w.last_w is not None:
                add(w.last_w)
            for key, v in w.readers.items():
                add((key[0], key[1], v))
        return deps

    def op(self, eng, fn, reads=(), writes=(), pe_acc=False):
        o = Op()
        o.eng = eng
        o.fn = fn
        o.signal = False
        o.dma_sem = None
        o.dma_cnt = 0
        idx = len(self.ops[eng])
        deps = self._deps(reads, writes)
        if pe_acc and eng == "pe":
            deps.pop(("eng", "pe"), None)
        o.waits = deps
        self.ops[eng].append(o)
        ev = ("eng", eng, idx)
        for r in reads:
            _add_reader(r, ev)
        for w in writes:
            w.last_w = ev
            w.readers = {}
        return ev

    def dma(self, queue, fn, sem, reads=(), writes=(), is_output=False):
        o = Op()
        o.eng = queue
        o.fn = fn
        o.signal = False
        o.waits = self._deps(reads, writes)
        self.dma_counts[sem] += 16
        o.dma_sem = sem
        o.dma_cnt = self.dma_counts[sem]
        self.ops[queue].append(o)
        ev = ("dma", sem, o.dma_cnt)
        for r in reads:
            _add_reader(r, ev)
        for w in writes:
            w.last_w = ev
            w.readers = {}
        if is_output:
            self.out_events.append(ev)
        return ev

    def emit(self, final_eng="sp"):
        nc = self.nc
        fin = Op()
        fin.eng = final_eng
        fin.fn = None
        fin.signal = False
        fin.dma_sem = None
        fin.dma_cnt = 0
        fw = {}
        for ev in self.out_events:
            key = (ev[0], ev[1])
            fw[key] = max(fw.get(key, -1), ev[2])
        fin.waits = fw
        self.ops[final_eng].append(fin)

        for e in ENGS:
            for o in self.ops[e]:
                for key, v in o.waits.items():
                    if key[0] == "eng":
                        self.ops[key[1]][v].signal = True
        rank = {}
        for e in ENGS:
            k = 0
            for i, o in enumerate(self.ops[e]):
                if o.signal:
                    k += 1
                    rank[(e, i)] = k

        with contextlib.ExitStack() as st:
            esem = {e: st.enter_context(nc.semaphore("es_" + e)) for e in ENGS}
            dsem = [st.enter_context(nc.semaphore("ds_%d" % i)) for i in range(self.n_dma_sems)]
            block = st.enter_context(nc.Block())

            def run_engine(e, handle):
                seen = {}
                for o in self.ops[e]:
                    for key, v in o.waits.items():
                        if key[0] == "eng":
                            c = rank[(key[1], v)]
                            sem = esem[key[1]]
                        else:
                            c = v
                            sem = dsem[key[1]]
                        if c > seen.get(key, 0):
                            handle.wait_ge(sem, c)
                            seen[key] = c
                    if o.fn is None:
                        continue
                    ins = None
                    for (meth, a, kw) in o.fn:
                        ins = getattr(handle, meth)(*a, **kw)
                    if o.dma_sem is not None:
                        ins.then_inc(dsem[o.dma_sem], 16)
                    elif o.signal:
                        ins.then_inc(esem[e], 1)

            @block.tensor
            def _(h):
                run_engine("pe", h)

            @block.scalar
            def _(h):
                run_engine("act", h)

            @block.vector
            def _(h):
                run_engine("dve", h)

            @block.gpsimd
            def _(h):
                run_engine("pool", h)

            @block.sync
            def _(h):
                run_engine("sp", h)


def I(meth, *a, **kw):
    return (meth, a, kw)


class _Stop(Exception):
    pass


def build_program(dbg=None, stop=0):
    nc = bass.Bass("TRN2", target_bir_lowering=False)
    xT = nc.dram_tensor("xT", [D, S_LEN], F32, kind="ExternalInput").ap()
    w_in = nc.dram_tensor("w_in", [D, IN_W], F32, kind="ExternalInput").ap()
    pool_w = nc.dram_tensor("pool_w", [1024, 512], F32, kind="ExternalInput").ap()
    w_al = nc.dram_tensor("w_al", [17, 1024], F32, kind="ExternalInput").ap()
    w_gla = nc.dram_tensor("w_gla", [D, D], F32, kind="ExternalInput").ap()
    w_out = nc.dram_tensor("w_out", [D, D], F32, kind="ExternalInput").ap()
    w_up = nc.dram_tensor("w_up", [D, DFF], F32, kind="ExternalInput").ap()
    w_dn = nc.dram_tensor("w_dn", [DFF, D], F32, kind="ExternalInput").ap()
    vecs_d = nc.dram_tensor("vecs", [128, 80], F32, kind="ExternalInput").ap()
    cst_d = nc.dram_tensor("cst", [128, C_W], F32, kind="ExternalInput").ap()
    outT = nc.dram_tensor("outT", [D, S_LEN], F32, kind="ExternalOutput").ap()
    dbg_t = {}
    if dbg:
        for name, shape in dbg.items():
            dbg_t[name] = nc.dram_tensor("dbg_" + name, list(shape), F32, kind="ExternalOutput").ap()

    P = Prog(nc)
    with contextlib.ExitStack() as st:
        base = (int(nc.sbuf_base) + 63) // 64 * 64
        ARENA = 210432
        st.enter_context(nc.sbuf_tensor("arena", [128, ARENA + 64], mybir.dt.uint8))
        cnt = [0]

        def at(off, shape, dt, nm):
            cnt[0] += 1
            return nc.alloc_sbuf_tensor_at("%s_%d" % (nm, cnt[0]), list(shape), dt, offset=base + off)

        OFF_HT, OFF_W, OFF_A1, OFF_A2, OFF_B0, OFF_B1 = 0, 32768, 81920, 114688, 147456, 163840
        OFF_S, OFF_SB, OFF_RSTD, OFF_M = 180224, 196608, 200704, 204800

        hT = at(OFF_HT, [128, KC, TH], BF16, "hT")
        Wr = [at(OFF_W + 16384 * i, [128, 8192], BF16, "Wr") for i in range(3)]
        Sst = at(OFF_S, [128, 4, 2, 512], F32, "S")
        Sb = [at(OFF_SB + 2048 * i, [128, 2, 512], BF16, "Sb") for i in range(2)]
        rstd = at(OFF_RSTD, [128, TH], F32, "rstd")
        o = OFF_M
        vecs = at(o, [128, 80], F32, "vecs"); o += 320
        ones = at(o, [128, 128], F32R, "ones"); o += 512
        triM = at(o, [128, 128], F32R, "triM"); o += 512
        ind = at(o, [128, 2], F32R, "ind"); o += 32
        invc = at(o, [128, 4, 16], F32, "invc"); o += 256
        ident = at(o, [128, 128], BF16, "ident"); o += 256
        Wa = at(o, [128, KC, 16], BF16, "Wa"); o += 512
        cd = at(o, [128, 128], F32, "cd"); o += 512
        smalls = at(o, [128, 16], F32, "smalls"); o += 64
        uhist = at(o, [128, 8, 16], F32, "uhist"); o += 512
        tmp16 = at(o, [128, 16], F32, "tmp16"); o += 64
        poolw = at(o, [128, 2, 512], BF16, "poolw"); o += 2048
        assert o <= ARENA, o

        xs = [at(OFF_B0 + 4096 * i, [128, TH], F32, "xs") for i in range(3)]
        sq = [at(OFF_B1 + 4096 * i, [128, TH], F32R, "sq") for i in range(2)]
        rl = [at(OFF_B1 + 8192 + 2048 * i, [128, 512], F32, "rl") for i in range(2)]
        walb = at(OFF_B0 + 0, [17, 1024], BF16, "walb")
        alow = at(OFF_B0 + 2048, [17, TH], BF16, "alow")
        e_t = [at(OFF_B0 + 4096 + 1024 * i, [128, 256], F32, "e_t") for i in range(2)]
        sp_t = [at(OFF_B0 + 6144 + 1024 * i, [128, 256], F32R, "sp_t") for i in range(2)]
        w_t = [at(OFF_B0 + 8192 + 1024 * i, [128, 256], F32, "w_t") for i in range(2)]
        ofm = [at(OFF_B0 + 10240 + 1024 * i, [128, 512], BF16, "ofm") for i in range(2)]
        junk = at(OFF_B0 + 12288, [128, 512], BF16, "junk")
        qT2 = [at(OFF_A2 + 0, [128, 2, TH], BF16, "qTa"), at(OFF_A2 + 24576, [128, 2, TH], BF16, "qTb")]
        kdec2 = [at(OFF_A2 + 4096, [128, 8, 256], BF16, "kdeca"), at(OFF_A2 + 28672, [128, 8, 256], BF16, "kdecb")]
        vtm2 = [at(OFF_A2 + 8192, [128, 8, 512], BF16, "vtma"), at(OFF_B1 + 0, [128, 8, 512], BF16, "vtmb")]
        sgt2 = [at(OFF_A2 + 16384, [128, 8, 512], BF16, "sgta"), at(OFF_B1 + 8192, [128, 8, 512], BF16, "sgtb")]
        ofT = at(OFF_A1, [128, KC, TH], BF16, "ofT")
        u_ext = at(OFF_A2 + 0, [128, 16 + TH], F32, "u_ext")
        ta = at(OFF_A2 + 4160, [128, 16 + TH], F32, "ta")
        tb = at(OFF_A2 + 8320, [128, 16 + TH], F32, "tb")
        t1 = at(OFF_A2 + 12480, [128, 4, TH], F32, "t1")
        s2t = at(OFF_A2 + 28864, [128, 512], F32, "s2t")
        d_g = at(OFF_SB, [128, 2, TH], BF16, "d_g")
        sqo = at(OFF_SB, [128, TH], F32R, "sqo")
        mixT = at(OFF_B0, [128, KC, TH], BF16, "mixT")
        x1T = at(OFF_A1, [128, KC, TH], F32, "x1T")
        hid = [at(OFF_B0 + 8192 * i, [128, 4, TH], BF16, "hid") for i in range(2)]
        orng = [at(OFF_B0 + 4096 * i, [128, TH], F32, "orng") for i in range(4)]

        ps = [st.enter_context(nc.psum_tensor("ps%d" % i, [128, 512], F32)) for i in range(7)]
        psT = st.enter_context(nc.psum_tensor("psT", [128, 1024], BF16))
        r_ps = [Res("ps%d" % i) for i in range(7)]
        r_psT = Res("psT")
        bank_ctr = [0]

        nb_excl = set()

        def nb():
            while True:
                b = bank_ctr[0] % 7
                bank_ctr[0] += 1
                if b not in nb_excl:
                    return b

        r_hT = [Res("hT%d" % k) for k in range(KC)]
        r_W = [Res("W%d" % i) for i in range(3)]
        r_S = [[Res("S%d%d" % (h, j)) for j in range(2)] for h in range(4)]
        r_rstd = Res("rstd")
        r_vecs, r_ones, r_tri, r_ind, r_invc, r_ident, r_Wa = (Res(n) for n in ("vecs", "ones", "tri", "ind", "invc", "ident", "Wa"))
        r_cd = [Res("cd%d" % h) for h in range(4)]
        r_small = Res("smalls")
        r_uhist = [Res("uh%d" % i) for i in range(8)]
        r_tmp16 = Res("tmp16")
        r_poolw = Res("poolw")

        occ = {"A1": [], "A2": [], "B0": [], "B1": [], "SB": []}

        def enter(regions, new):
            seed = {}
            for rg in regions:
                for r in occ[rg]:
                    if r.last_w is not None:
                        k = (r.last_w[0], r.last_w[1])
                        seed[k] = max(seed.get(k, -1), r.last_w[2])
                    for k, v in r.readers.items():
                        seed[k] = max(seed.get(k, -1), v)
            for r in new:
                r.readers = dict(seed)
            for rg in regions:
                occ[rg] = list(new)

        wsem = [P.new_dma_sem() for _ in range(3)]
        ws = {"plan": [], "issued": 0, "released": 0, "acq": 0}

        def Wv(s, ncols=512):
            return Wr[s][:, 0:KC * ncols].rearrange("p (k c) -> p k c", c=ncols)

        def ws_issue():
            while ws["issued"] < len(ws["plan"]) and ws["issued"] < ws["released"] + 3:
                n = ws["issued"]
                s = n % 3
                for f in ws["plan"][n]:
                    P.dma("pool", [f(s)], wsem[s], writes=[r_W[s]])
                ws["issued"] += 1

        def ws_acquire():
            n = ws["acq"]
            ws["acq"] += 1
            assert n < ws["issued"], "weight load not issued"
            return n % 3

        def ws_release():
            ws["released"] += 1
            ws_issue()

        def ld_cols(src, c0, ncols=512):
            def f(s):
                return I("dma_start", out=Wv(s, ncols), in_=src[:, c0:c0 + ncols].rearrange("(k p) c -> p k c", p=128))
            return f

        def ld_qk(hh):
            def fq(s):
                return I("dma_start", out=Wv(s)[:, :, 0:256],
                         in_=w_in[:, Q0 + hh * 256:Q0 + hh * 256 + 256].rearrange("(k p) c -> p k c", p=128))

            def fk(s):
                return I("dma_start", out=Wv(s)[:, :, 256:512],
                         in_=w_in[:, K0 + hh * 256:K0 + hh * 256 + 256].rearrange("(k p) c -> p k c", p=128))
            return [fq, fk]

        def ld_dn(fg):
            def f(s):
                return I("dma_start", out=Wr[s][:, :].rearrange("p (f c) -> p f c", c=2048),
                         in_=w_dn[fg * 512:(fg + 1) * 512, :].rearrange("(f p) c -> p f c", p=128))
            return f

        for hf in range(NH):
            for hh in range(4):
                ws["plan"].append(ld_qk(hh))
                ws["plan"].append([ld_cols(w_in, V0 + hh * 512)])
                ws["plan"].append([ld_cols(w_in, G0 + hh * 512)])
            for cg in range(4):
                ws["plan"].append([ld_cols(w_in, U0 + cg * 256, 256)])
                ws["plan"].append([ld_cols(w_in, GP0 + cg * 512)])
                ws["plan"].append([ld_cols(w_gla, cg * 512)])
                ws["plan"].append([ld_cols(w_in, GG0 + cg * 512)])
            for cg in range(4):
                ws["plan"].append([ld_cols(w_out, cg * 512)])
            ws["plan"].append([ld_cols(w_up, 0)])
            for fg in range(16):
                if fg + 1 < 16:
                    ws["plan"].append([ld_cols(w_up, (fg + 1) * 512)])
                ws["plan"].append([ld_dn(fg)])

        def const_load(q, dst, src, res):
            P.dma(q, [I("dma_start", out=dst, in_=src)], P.new_dma_sem(), writes=[res])

        const_load("sp", vecs[:, :], vecs_d[:, :], r_vecs)
        stg = at(OFF_RSTD, [128, 258], F32, "cstg")
        const_load("sp", stg[:, :], cst_d[:, C_ONES:C_ONES + 258], r_rstd)
        P.op("dve", [I("tensor_copy", out=ones[:, :], in_=stg[:, 0:128])], reads=[r_rstd], writes=[r_ones])
        P.op("dve", [I("tensor_copy", out=triM[:, :], in_=stg[:, 128:256])], reads=[r_rstd], writes=[r_tri])
        P.op("dve", [I("tensor_copy", out=ind[:, :], in_=stg[:, 256:258])], reads=[r_rstd], writes=[r_ind])
        const_load("sp", invc[:, :, :], cst_d[:, C_INV:C_INV + 64].rearrange("p (g t) -> p g t", t=16), r_invc)
        const_load("pool", ident[:, :], cst_d[:, C_ID:C_ID + 128], r_ident)
        const_load("pool", Wa[:, :, :], w_in[:, AL0:AL0 + 16].rearrange("(k p) c -> p k c", p=128), r_Wa)
        P.op("dve", [I("memset", Sst[:, :, :, :], 0.0)], writes=[r for hh in r_S for r in hh])
        P.op("dve", [I("memset", uhist[:, :, :], 0.0)], writes=r_uhist)

        s_x = [P.new_dma_sem() for _ in range(3)]
        s_x1 = [P.new_dma_sem() for _ in range(KC)]
        s_out = [P.new_dma_sem() for _ in range(4)]
        s_pw = P.new_dma_sem()
        s_wal = P.new_dma_sem()
        s_dbg = P.new_dma_sem()

        def dbg_dump(name, src_ap, reads):
            if name in dbg_t:
                P.dma("pool", [I("dma_start", out=dbg_t[name], in_=src_ap)], P.new_dma_sem(), reads=reads, is_output=True)

        def rms_rstd(src_list, src_res_list, r_sq, act_only=False):
            b0, b1 = nb(), nb()
            for k in range(KC):
                sl = k % 2
                if act_only or k % 2 == 0:
                    P.op("act", [I("activation", out=sq[sl][:, :], in_=src_list[k], func=AF.Square)],
                         reads=src_res_list[k], writes=[r_sq[sl]])
                else:
                    P.op("dve", [I("tensor_tensor", out=sq[sl][:, :], in0=src_list[k], in1=src_list[k], op=ALU.mult)],
                         reads=src_res_list[k], writes=[r_sq[sl]])
                P.op("pe", [I("matmul", ps[b][:, :], ones[:, :], sq[sl][:, j * 512:(j + 1) * 512], start=(k == 0), stop=(k == KC - 1))
                            for j, b in enumerate((b0, b1))],
                     reads=[r_sq[sl], r_ones], writes=[r_ps[b0], r_ps[b1]], pe_acc=True)
            for j, b in enumerate((b0, b1)):
                P.op("act", [I("activation", out=rstd[:, j * 512:(j + 1) * 512], in_=ps[b][:, :], func=AF.Ln, bias=EPS, scale=1.0 / D)],
                     reads=[r_ps[b]], writes=[r_rstd])
            P.op("act", [I("activation", out=rstd[:, :], in_=rstd[:, :], func=AF.Exp, scale=-0.5)], reads=[r_rstd], writes=[r_rstd])

        def proj_fm(s, m, ncols, rhs, rhs_res, nk=KC):
            b0, b1 = nb(), nb()
            wv = Wv(s, ncols)
            P.op("pe", [I("matmul", ps[b][:, :], wv[:, k, m * 128:(m + 1) * 128], rhs[:, k, j * 512:(j + 1) * 512],
                          start=(k == 0), stop=(k == nk - 1)) for k in range(nk) for j, b in enumerate((b0, b1))],
                 reads=[r_W[s]] + rhs_res, writes=[r_ps[b0], r_ps[b1]], pe_acc=True)
            return b0, b1

        pre_xa = []

        def run_halves():
          for hf in range(NH):
            t0 = hf * TH
            r_sq = [Res("sq%d" % i) for i in range(2)]
            enter(["B1"], r_sq)
            if len(pre_xa) == KC:
                r_xa = list(pre_xa)
                del pre_xa[:]
                occ["A1"] = [r[0] for r in r_xa]
                occ["A2"] = [r[0] for r in r_xa]
            else:
                r_xa = [[Res("xa%d" % k)] for k in range(KC)]
                enter(["A1", "A2"], [r[0] for r in r_xa])
                for k in range(KC):
                    P.dma("sp" if k % 2 == 0 else "act", [I("dma_start", out=x1T[:, k, :], in_=xT[k * 128:(k + 1) * 128, t0:t0 + TH])],
                          s_x1[k], writes=r_xa[k])
            if hf == 0:
                for i_ in range(3):
                    r_W[i_].readers = dict(r_xa[KC - 1 - i_][0].last_w and {(r_xa[KC - 1 - i_][0].last_w[0], r_xa[KC - 1 - i_][0].last_w[1]): r_xa[KC - 1 - i_][0].last_w[2]})
                ws_issue()
            rms_rstd([x1T[:, k, :] for k in range(KC)], r_xa, r_sq, act_only=(hf > 0))
            for k in range(KC):
                P.op("dve", [I("scalar_tensor_tensor", out=hT[:, k, :], in0=x1T[:, k, :], scalar=vecs[:, VG1 + k:VG1 + k + 1],
                               in1=rstd[:, :], op0=ALU.mult, op1=ALU.mult)],
                     reads=r_xa[k] + [r_vecs, r_rstd], writes=[r_hT[k]])
            if hf == 0:
                dbg_dump("hT", hT[:, :, :], r_hT)

            if stop == 1:
                raise _Stop()
            r_walb, r_alow = Res("walb"), Res("alow")
            r_e = [Res("e0"), Res("e1")]
            r_sp = [Res("sp0"), Res("sp1")]
            r_w = [Res("w0"), Res("w1")]
            r_ofm = [Res("ofm0"), Res("ofm1")]
            enter(["B0"], [r_walb, r_alow] + r_e + r_sp + r_w + r_ofm)
            r_qT = [[Res("qT%d_%d" % (p_, m)) for m in range(2)] for p_ in range(2)]
            r_kd = [[Res("kd%d_%d" % (p_, i)) for i in range(8)] for p_ in range(2)]
            r_v = [[Res("v%d_%d" % (p_, i)) for i in range(8)] for p_ in range(2)]
            r_sg = [[Res("sg%d_%d" % (p_, i)) for i in range(8)] for p_ in range(2)]
            enter(["A2"], r_qT[0] + r_qT[1] + r_kd[0] + r_kd[1] + r_v[0] + r_sg[0])
            enter(["B1"], r_v[1] + r_sg[1])
            r_ofT = [Res("ofT%d" % i) for i in range(KC)]
            enter(["A1"], r_ofT)
            r_Sb = [Res("Sb0"), Res("Sb1")]
            enter(["SB"], r_Sb)

            P.dma("pool", [I("dma_start", out=walb[:, :], in_=w_al[:, :])], s_wal, writes=[r_walb])
            P.op("dve", [I("memset", alow[:, :], 1.0)], writes=[r_alow])
            b0, b1 = nb(), nb()
            for k in range(KC):
                P.op("pe", [I("matmul", ps[b][0:16, :], Wa[:, k, :], hT[:, k, j * 512:(j + 1) * 512], start=(k == 0), stop=(k == KC - 1))
                            for j, b in enumerate((b0, b1))],
                     reads=[r_Wa, r_hT[k]], writes=[r_ps[b0], r_ps[b1]], pe_acc=True)
            for j, b in enumerate((b0, b1)):
                P.op("act", [I("activation", out=alow[0:16, j * 512:(j + 1) * 512], in_=ps[b][0:16, :], func=AF.Copy)],
                     reads=[r_ps[b]], writes=[r_alow])

            pool_ctr = [0]

            def nbp():
                b = 3 + pool_ctr[0] % 4
                pool_ctr[0] += 1
                return b

            def make_prep(hh):
                pb = hh % 2
                stt = {}

                def Z(tt):
                    i2 = tt % 2
                    bz = nbp()
                    P.op("pe", [I("matmul", ps[bz][:, 0:256], alow[0:17, tt * 128:(tt + 1) * 128], walb[0:17, hh * 256:(hh + 1) * 256],
                                  start=True, stop=True)],
                         reads=[r_alow, r_walb], writes=[r_ps[bz]], pe_acc=True)
                    P.op("act", [I("activation", out=e_t[i2][:, :], in_=ps[bz][:, 0:256], func=AF.Exp, scale=-1.0)],
                         reads=[r_ps[bz]], writes=[r_e[i2]])
                    P.op("act", [I("activation", out=sp_t[i2][:, :], in_=e_t[i2][:, :], func=AF.Ln, bias=1.0)],
                         reads=[r_e[i2]], writes=[r_sp[i2]])

                def pq(m):
                    def f():
                        if m == 0:
                            stt["qk"] = ws_acquire()
                        s = stt["qk"]
                        ba, bb = nbp(), nbp()
                        wv = Wv(s)
                        P.op("pe", [I("matmul", ps[b][:, :], wv[:, k, m * 128:(m + 1) * 128], hT[:, k, j * 512:(j + 1) * 512],
                                      start=(k == 0), stop=(k == KC - 1)) for k in range(KC) for j, b in enumerate((ba, bb))],
                             reads=[r_W[s]] + r_hT, writes=[r_ps[ba], r_ps[bb]], pe_acc=True)
                        for j, b in enumerate((ba, bb)):
                            P.op("act", [I("activation", out=qT2[pb][:, m, j * 512:(j + 1) * 512], in_=ps[b][:, :], func=AF.Copy, scale=1.0 / 16.0)],
                                 reads=[r_ps[b]], writes=[r_qT[pb][m]])
                    return f

                def pk(tt):
                    def f():
                        s = stt["qk"]
                        wqk = Wv(s)
                        i2 = tt % 2
                        tsl = slice(tt * 128, (tt + 1) * 128)
                        if tt == 0:
                            Z(0)
                        bk = nbp()
                        P.op("pe", [I("matmul", ps[bk][:, 0:256], hT[:, k, tsl], wqk[:, k, 256:512], start=(k == 0), stop=(k == KC - 1))
                                    for k in range(KC)],
                             reads=[r_W[s]] + r_hT, writes=[r_ps[bk]], pe_acc=True)
                        if tt + 1 < 8:
                            Z(tt + 1)
                        brv = nbp()
                        P.op("pe", [I("matmul", ps[brv][:, 0:256], triM[:, :], sp_t[i2][:, :], start=True, stop=True)]
                                   + [I("matmul", ps[brv][:, 256 + 2 * j:256 + 2 * j + 2], sp_t[i2][:, j * 128:(j + 1) * 128], ind[:, :],
                                        start=True, stop=True) for j in range(2)],
                             reads=[r_tri, r_ind, r_sp[i2]], writes=[r_ps[brv]], pe_acc=True)
                        P.op("act", [I("activation", out=cd[:, (hh * 2 + j) * 16 + 2 * tt:(hh * 2 + j) * 16 + 2 * tt + 2],
                                       in_=ps[brv][:, 256 + 2 * j:256 + 2 * j + 2], func=AF.Exp) for j in range(2)]
                                    + [I("activation", out=w_t[i2][:, :], in_=ps[brv][:, 0:256], func=AF.Exp)],
                             reads=[r_ps[brv]], writes=[r_cd[hh], r_w[i2]])
                        P.op("dve", [I("tensor_tensor", out=kdec2[pb][:, tt, :], in0=ps[bk][:, 0:256], in1=w_t[i2][:, :], op=ALU.mult)],
                             reads=[r_ps[bk], r_w[i2]], writes=[r_kd[pb][tt]])
                        if tt == 7:
                            ws_release()
                    return f

                def pv(tt):
                    def f():
                        if tt == 0:
                            stt["v"] = ws_acquire()
                        s = stt["v"]
                        wvv = Wv(s)
                        bv = nbp()
                        P.op("pe", [I("matmul", ps[bv][:, :], hT[:, k, tt * 128:(tt + 1) * 128], wvv[:, k, :], start=(k == 0), stop=(k == KC - 1))
                                    for k in range(KC)],
                             reads=[r_W[s]] + r_hT, writes=[r_ps[bv]], pe_acc=True)
                        P.op("dve", [I("tensor_copy", out=vtm2[pb][:, tt, :], in_=ps[bv][:, :])], reads=[r_ps[bv]], writes=[r_v[pb][tt]])
                        if tt == 7:
                            ws_release()
                    return f

                def pg(tt):
                    def f():
                        if tt == 0:
                            stt["g"] = ws_acquire()
                        s = stt["g"]
                        wgg = Wv(s)
                        bg = nbp()
                        P.op("pe", [I("matmul", ps[bg][:, :], hT[:, k, tt * 128:(tt + 1) * 128], wgg[:, k, :], start=(k == 0), stop=(k == KC - 1))
                                    for k in range(KC)],
                             reads=[r_W[s]] + r_hT, writes=[r_ps[bg]], pe_acc=True)
                        P.op("act", [I("activation", out=sgt2[pb][:, tt, :], in_=ps[bg][:, :], func=AF.Silu)],
                             reads=[r_ps[bg]], writes=[r_sg[pb][tt]])
                        if tt == 7:
                            ws_release()
                    return f

                return [pq(0), pq(1)] + [pk(tt) for tt in range(8)] + [pv(tt) for tt in range(8)] + [pg(tt) for tt in range(8)]

            r_sm2 = [Res("smalls_l0"), Res("smalls_l1")]

            def rec_kv(hh, c, lane=None):
                pb = hh % 2
                tt, ph = c // 2, c % 2
                prt = slice(ph * 64, (ph + 1) * 64)
                kb = (0, 1) if not lane else (3, 4)
                P.op("pe", [I("matmul", ps[kb[j]][:, :], kdec2[pb][prt, tt, j * 128:(j + 1) * 128], vtm2[pb][prt, tt, :], start=True, stop=True)
                            for j in range(2)],
                     reads=[r_kd[pb][tt], r_v[pb][tt]], writes=[r_ps[kb[0]], r_ps[kb[1]]], pe_acc=True)
                rb = (c % 2) if lane is None else lane
                for j in range(2):
                    ci = (hh * 2 + j) * 16 + c
                    P.op("dve", [I("scalar_tensor_tensor", out=Sst[:, hh, j, :], in0=Sst[:, hh, j, :], scalar=cd[:, ci:ci + 1],
                                   in1=ps[kb[j]][:, :], op0=ALU.mult, op1=ALU.add)],
                         reads=[r_S[hh][j], r_cd[hh], r_ps[kb[j]]], writes=[r_S[hh][j]])
                if lane is None:
                    rec_cast(hh, c, lane)

            def rec_cast(hh, c, lane=None):
                rb = (c % 2) if lane is None else lane
                P.op("act", [I("activation", out=Sb[rb][:, :, :], in_=Sst[:, hh, :, :], func=AF.Copy)],
                     reads=[r_S[hh][0], r_S[hh][1]], writes=[r_Sb[rb]])

            def rec_o(hh, c, lane=None):
                pb = hh % 2
                tt, ph = c // 2, c % 2
                prt = slice(ph * 64, (ph + 1) * 64)
                rb = (c % 2) if lane is None else lane
                bO = 2 if not lane else 5
                P.op("pe", [I("matmul", ps[bO][prt, :], qT2[pb][:, j, c * 64:(c + 1) * 64], Sb[rb][:, j, :], start=(j == 0), stop=(j == 1))
                            for j in range(2)],
                     reads=[r_qT[pb][0], r_qT[pb][1], r_Sb[rb]], writes=[r_ps[bO]], pe_acc=True)
                if ph == 1:
                    i2 = (tt % 2) if lane is None else lane
                    sc = 3 * i2
                    rs_ = r_sm2[0] if not lane else r_sm2[1]
                    P.op("act", [I("activation", out=junk[:, :], in_=ps[bO][:, :], func=AF.Square, accum_out=smalls[:, sc:sc + 1])],
                         reads=[r_ps[bO]], writes=[rs_])
                    P.op("act", [I("activation", out=smalls[:, sc + 1:sc + 2], in_=smalls[:, sc:sc + 1], func=AF.Sqrt, bias=EPS, scale=1.0 / 512.0)],
                         reads=[rs_], writes=[rs_])
                    P.op("dve", [I("reciprocal", out=smalls[:, sc + 2:sc + 3], in_=smalls[:, sc + 1:sc + 2])],
                         reads=[rs_], writes=[rs_])
                    P.op("dve", [I("scalar_tensor_tensor", out=ofm[i2][:, :], in0=ps[bO][:, :], scalar=smalls[:, sc + 2:sc + 3],
                                   in1=sgt2[pb][:, tt, :], op0=ALU.mult, op1=ALU.mult)],
                         reads=[r_ps[bO], rs_, r_sg[pb][tt]], writes=[r_ofm[i2]])

            def rec_tr(hh, c, lane=None):
                tt = c // 2
                i2 = (tt % 2) if lane is None else lane
                P.op("pe", [I("transpose", psT[:, i * 128:(i + 1) * 128], ofm[i2][:, i * 128:(i + 1) * 128], ident[:, :]) for i in range(4)],
                     reads=[r_ofm[i2], r_ident], writes=[r_psT], pe_acc=True)
                P.op("act", [I("activation", out=ofT[:, hh * 4 + i, tt * 128:(tt + 1) * 128], in_=psT[:, i * 128:(i + 1) * 128],
                               func=AF.Copy, scale=vecs[:, VGN + hh * 4 + i:VGN + hh * 4 + i + 1]) for i in range(4)],
                     reads=[r_psT, r_vecs], writes=[r_ofT[hh * 4 + i] for i in range(4)])

            queue = list(make_prep(0))
            while queue:
                queue.pop(0)()
            if hf == 0:
                dbg_dump("qT", qT2[0][:, :, :], r_qT[0])
                dbg_dump("kdec", kdec2[0][:, :, :], r_kd[0])
                dbg_dump("vtm", vtm2[0][:, :, :], r_v[0])
                dbg_dump("sgt", sgt2[0][:, :, :], r_sg[0])
                dbg_dump("cd", cd[:, 0:32], [r_cd[0]])
            if stop == 2:
                raise _Stop()
            for hh in range(2):
                queue = list(make_prep(hh + 1))
                pend_tr = []
                for c in range(16):
                    rec_kv(hh, c)
                    if queue:
                        queue.pop(0)()
                    if pend_tr and pend_tr[0] <= c - 3:
                        rec_tr(hh, pend_tr.pop(0))
                    if c >= 1:
                        rec_o(hh, c - 1)
                        if (c - 1) % 2 == 1:
                            pend_tr.append(c - 1)
                    if queue:
                        queue.pop(0)()
                rec_o(hh, 15)
                pend_tr.append(15)
                while queue:
                    queue.pop(0)()
                for c_ in pend_tr:
                    rec_tr(hh, c_)
            queue = list(make_prep(3))
            while queue:
                queue.pop(0)()
            lanes = ((0, 2), (1, 3))
            pend2 = {0: [], 1: []}
            for s_ in range(18):
                for ln, hh in lanes:
                    c = s_ - ln
                    if 0 <= c < 16:
                        rec_kv(hh, c, lane=ln)
                for ln, hh in lanes:
                    c = s_ - ln
                    if pend2[ln] and pend2[ln][0] <= c - 2:
                        rec_tr(hh, pend2[ln].pop(0), lane=ln)
                for ln, hh in lanes:
                    c = s_ - ln
                    if 1 <= c <= 16:
                        rec_o(hh, c - 1, lane=ln)
                        if (c - 1) % 2 == 1:
                            pend2[ln].append(c - 1)
                for ln, hh in lanes:
                    c = s_ - ln
                    if 0 <= c < 16:
                        rec_cast(hh, c, lane=ln)
            for ln, hh in lanes:
                for c_ in pend2[ln]:
                    rec_tr(hh, c_, lane=ln)
            if stop == 3:
                raise _Stop()
            if hf == 0:
                dbg_dump("ofT", ofT[:, :, :], r_ofT)

            r_u, r_ta, r_tb, r_s2 = Res("u_ext"), Res("ta"), Res("tb"), Res("s2t")
            r_t1 = [[Res("t1_%d%d" % (m, j)) for j in range(2)] for m in range(4)]
            enter(["A2"], [r_u, r_ta, r_tb, r_s2] + [r for mm_ in r_t1 for r in mm_])
            r_dg = [Res("dg0"), Res("dg1")]
            enter(["SB"], r_dg)
            r_mix = [Res("mix%d" % i) for i in range(KC)]
            enter(["B0", "B1"], r_mix)
            L = 16 + TH

            for cg in range(4):
                wdw = POOL_WINDOWS[cg]
                P.dma("pool", [I("dma_start", out=poolw[:, :, :], in_=pool_w[cg * 256:(cg + 1) * 256, :].rearrange("(k p) c -> p k c", p=128))],
                      s_pw, writes=[r_poolw])
                s_u = ws_acquire()
                for ct in range(2):
                    cti = cg * 2 + ct
                    bu = proj_fm(s_u, ct, 256, hT, r_hT)
                    P.op("dve", [I("tensor_copy", out=u_ext[:, 0:16], in_=uhist[:, cti, :])], reads=[r_uhist[cti]], writes=[r_u])
                    for j, b in enumerate(bu):
                        P.op("act", [I("activation", out=u_ext[:, 16 + j * 512:16 + (j + 1) * 512], in_=ps[b][:, :], func=AF.Copy)],
                             reads=[r_ps[b]], writes=[r_u])
                    P.op("dve", [I("tensor_copy", out=uhist[:, cti, :], in_=u_ext[:, TH:TH + 16])], reads=[r_u], writes=[r_uhist[cti]])
                    P.op("dve", [I("tensor_tensor", out=ta[:, 1:L], in0=u_ext[:, 1:L], in1=u_ext[:, 0:L - 1], op=ALU.add)],
                         reads=[r_u], writes=[r_ta])
                    cur, r_cur = ta, r_ta
                    if wdw >= 4:
                        P.op("dve", [I("tensor_tensor", out=tb[:, 3:L], in0=ta[:, 3:L], in1=ta[:, 1:L - 2], op=ALU.add)],
                             reads=[r_ta], writes=[r_tb])
                        cur, r_cur = tb, r_tb
                    if wdw >= 8:
                        P.op("dve", [I("tensor_tensor", out=ta[:, 7:L], in0=tb[:, 7:L], in1=tb[:, 3:L - 4], op=ALU.add)],
                             reads=[r_tb], writes=[r_ta])
                        cur, r_cur = ta, r_ta
                    if wdw >= 16:
                        P.op("dve", [I("tensor_tensor", out=tb[:, 15:L], in0=ta[:, 15:L], in1=ta[:, 7:L - 8], op=ALU.add)],
                             reads=[r_ta], writes=[r_tb])
                        cur, r_cur = tb, r_tb
                    P.op("dve", [I("scalar_tensor_tensor", out=d_g[:, ct, :], in0=cur[:, 16:L], scalar=1.0 / wdw, in1=u_ext[:, 16:L],
                                   op0=ALU.mult, op1=ALU.subtract)],
                         reads=[r_cur, r_u], writes=[r_dg[ct]])
                    if hf == 0:
                        P.op("dve", [I("tensor_tensor", out=tmp16[:, :], in0=cur[:, 16:32], in1=invc[:, cg, :], op=ALU.mult)],
                             reads=[r_cur, r_invc], writes=[r_tmp16])
                        P.op("dve", [I("tensor_tensor", out=d_g[:, ct, 0:16], in0=tmp16[:, :], in1=u_ext[:, 16:32], op=ALU.subtract)],
                             reads=[r_tmp16, r_u], writes=[r_dg[ct]])
                ws_release()
                if hf == 0:
                    dbg_dump("d_g%d" % cg, d_g[:, :, :], r_dg)
                s_gp = ws_acquire()
                for m in range(4):
                    bg_ = proj_fm(s_gp, m, 512, hT, r_hT)
                    for j in range(2):
                        P.op("act", [I("activation", out=t1[:, m, j * 512:(j + 1) * 512], in_=ps[bg_[j]][:, :], func=AF.Sigmoid)],
                             reads=[r_ps[bg_[j]]], writes=[r_t1[m][j]])
                ws_release()
                for m in range(4):
                    dt_ = cg * 4 + m
                    by = (nb(), nb())
                    P.op("pe", [I("matmul", ps[b][:, :], poolw[:, k, m * 128:(m + 1) * 128], d_g[:, k, j * 512:(j + 1) * 512],
                                  start=(k == 0), stop=(k == 1)) for k in range(2) for j, b in enumerate(by)],
                         reads=[r_poolw] + r_dg, writes=[r_ps[by[0]], r_ps[by[1]]], pe_acc=True)
                    for j in range(2):
                        P.op("dve", [I("scalar_tensor_tensor", out=t1[:, m, j * 512:(j + 1) * 512], in0=ps[by[j]][:, :],
                                       scalar=vecs[:, VPS + dt_:VPS + dt_ + 1], in1=t1[:, m, j * 512:(j + 1) * 512],
                                       op0=ALU.mult, op1=ALU.mult)],
                             reads=[r_ps[by[j]], r_vecs, r_t1[m][j]], writes=[r_t1[m][j]])
                s_wg = ws_acquire()
                s_gg = ws_acquire()
                for m in range(4):
                    dt_ = cg * 4 + m
                    byg = proj_fm(s_wg, m, 512, ofT, r_ofT)
                    bgg = proj_fm(s_gg, m, 512, hT, r_hT)
                    for j in range(2):
                        P.op("act", [I("activation", out=s2t[:, :], in_=ps[bgg[j]][:, :], func=AF.Sigmoid)],
                             reads=[r_ps[bgg[j]]], writes=[r_s2])
                        P.op("dve", [I("tensor_tensor", out=s2t[:, :], in0=ps[byg[j]][:, :], in1=s2t[:, :], op=ALU.mult)],
                             reads=[r_ps[byg[j]], r_s2], writes=[r_s2])
                        P.op("dve", [I("tensor_tensor", out=mixT[:, dt_, j * 512:(j + 1) * 512], in0=t1[:, m, j * 512:(j + 1) * 512],
                                       in1=s2t[:, :], op=ALU.add)],
                             reads=[r_t1[m][j], r_s2], writes=[r_mix[dt_]])
                ws_release(); ws_release()
            if hf == 0:
                dbg_dump("mixT", mixT[:, :, :], r_mix)

            if stop == 4:
                raise _Stop()
            r_x1 = [[Res("x1_%d_%d" % (m, j)) for j in range(2)] for m in range(KC)]
            enter(["A1", "A2"], [r for mm_ in r_x1 for r in mm_])
            for m in range(KC):
                P.dma("sp", [I("dma_start", out=x1T[:, m, :], in_=xT[m * 128:(m + 1) * 128, t0:t0 + TH])], s_x1[m], writes=r_x1[m])
            r_sqo = Res("sqo")
            enter(["SB"], [r_sqo])
            nb_excl.update((5, 6))

            def stats_chunks(ks):
                for k in ks:
                    P.op("act", [I("activation", out=sqo[:, :], in_=x1T[:, k, :], func=AF.Square)], reads=r_x1[k], writes=[r_sqo])
                    P.op("pe", [I("matmul", ps[b][:, :], ones[:, :], sqo[:, j * 512:(j + 1) * 512], start=(k == 0), stop=(k == KC - 1))
                                for j, b in enumerate((5, 6))],
                         reads=[r_sqo, r_ones], writes=[r_ps[5], r_ps[6]], pe_acc=True)

            def stats_finish():
                for j, b in enumerate((5, 6)):
                    P.op("act", [I("activation", out=rstd[:, j * 512:(j + 1) * 512], in_=ps[b][:, :], func=AF.Ln, bias=EPS, scale=1.0 / D)],
                         reads=[r_ps[b]], writes=[r_rstd])
                P.op("act", [I("activation", out=rstd[:, :], in_=rstd[:, :], func=AF.Exp, scale=-0.5)], reads=[r_rstd], writes=[r_rstd])
                nb_excl.clear()

            for cg in range(4):
                s_o = ws_acquire()
                for m in range(4):
                    dt_ = cg * 4 + m
                    bo = proj_fm(s_o, m, 512, mixT, r_mix)
                    for j in range(2):
                        P.op("dve", [I("tensor_tensor", out=x1T[:, dt_, j * 512:(j + 1) * 512], in0=x1T[:, dt_, j * 512:(j + 1) * 512],
                                       in1=ps[bo[j]][:, :], op=ALU.add)],
                             reads=[r_ps[bo[j]], r_x1[dt_][j]], writes=[r_x1[dt_][j]])
                ws_release()
                for k in range(cg * 4, cg * 4 + 4):
                    P.op("dve", [I("tensor_scalar", out=hT[:, k, :], in0=x1T[:, k, :], scalar1=vecs[:, VG2 + k:VG2 + k + 1], scalar2=None,
                                   op0=ALU.mult)],
                         reads=r_x1[k] + [r_vecs], writes=[r_hT[k]])
                if cg >= 1:
                    stats_chunks(range((cg - 1) * 4, cg * 4))
            if hf == 0:
                dbg_dump("x1T", x1T[:, :, :], [r for mm_ in r_x1 for r in mm_])

            if stop == 5:
                raise _Stop()
            r_sq = [Res("sq2_%d" % i) for i in range(2)]
            r_rl = [Res("rl%d" % i) for i in range(2)]
            enter(["B1"], r_sq + r_rl)
            r_hid = [[Res("hid%d_%d" % (i, m)) for m in range(4)] for i in range(2)]
            enter(["B0"], [r for hh_ in r_hid for r in hh_])
            if hf == 0:
                dbg_dump("h2T", hT[:, :, :], r_hT)

            def mlp_up(fg):
                s_up = ws_acquire()
                hb = fg % 2
                for m in range(4):
                    bu = proj_fm(s_up, m, 512, hT, r_hT)
                    if fg == 0 and m == 0:
                        stats_chunks(range(12, 16))
                        stats_finish()
                    for j in range(2):
                        ri = j
                        P.op("act", [I("activation", out=rl[ri][:, :], in_=ps[bu[j]][:, :], func=AF.Relu)],
                             reads=[r_ps[bu[j]]], writes=[r_rl[ri]])
                        P.op("dve", [I("tensor_tensor", out=rl[ri][:, :], in0=rl[ri][:, :], in1=rstd[:, j * 512:(j + 1) * 512], op=ALU.mult)],
                             reads=[r_rl[ri], r_rstd], writes=[r_rl[ri]])
                        P.op("dve", [I("tensor_tensor", out=hid[hb][:, m, j * 512:(j + 1) * 512], in0=rl[ri][:, :], in1=rl[ri][:, :], op=ALU.mult)],
                             reads=[r_rl[ri]], writes=[r_hid[hb][m]])
                ws_release()

            def mlp_dn(fg):
                s_dn = ws_acquire()
                hb = fg % 2
                wd = Wr[s_dn][:, :].rearrange("p (f c) -> p f c", c=2048)
                for m in range(KC):
                    bd = (nb(), nb())
                    P.op("pe", [I("matmul", ps[b][:, :], wd[:, f, m * 128:(m + 1) * 128], hid[hb][:, f, j * 512:(j + 1) * 512],
                                  start=(f == 0), stop=(f == 3)) for f in range(4) for j, b in enumerate(bd)],
                         reads=[r_W[s_dn]] + r_hid[hb], writes=[r_ps[bd[0]], r_ps[bd[1]]], pe_acc=True)
                    for j in range(2):
                        P.op("dve", [I("tensor_tensor", out=x1T[:, m, j * 512:(j + 1) * 512], in0=x1T[:, m, j * 512:(j + 1) * 512],
                                       in1=ps[bd[j]][:, :], op=ALU.add)],
                             reads=[r_ps[bd[j]], r_x1[m][j]], writes=[r_x1[m][j]])
                ws_release()

            mlp_up(0)
            for fg in range(16):
                if fg + 1 < 16:
                    mlp_up(fg + 1)
                mlp_dn(fg)

            if stop == 6:
                raise _Stop()
            r_or = [Res("or%d" % i) for i in range(4)]
            enter(["B0"], r_or)
            rms_rstd([x1T[:, k, :] for k in range(KC)], r_x1, r_sq, act_only=True)
            for k in range(KC):
                oi = k % 4
                P.op("dve", [I("scalar_tensor_tensor", out=orng[oi][:, :], in0=x1T[:, k, :], scalar=vecs[:, VGF + k:VGF + k + 1],
                               in1=rstd[:, :], op0=ALU.mult, op1=ALU.mult)],
                     reads=r_x1[k] + [r_vecs, r_rstd], writes=[r_or[oi]])
                P.dma("sp", [I("dma_start", out=outT[k * 128:(k + 1) * 128, t0:t0 + TH], in_=orng[oi][:, :])],
                      s_out[oi], reads=[r_or[oi]], is_output=True)
                if hf + 1 < NH:
                    nr = Res("xa%d" % k)
                    seed = {}
                    for r in r_x1[k]:
                        if r.last_w is not None:
                            kk = (r.last_w[0], r.last_w[1])
                            seed[kk] = max(seed.get(kk, -1), r.last_w[2])
                        for kk, v in r.readers.items():
                            seed[kk] = max(seed.get(kk, -1), v)
                    nr.readers = seed
                    P.dma("sp", [I("dma_start", out=x1T[:, k, :], in_=xT[k * 128:(k + 1) * 128, t0 + TH:t0 + 2 * TH])], s_x1[k], writes=[nr])
                    pre_xa.append([nr])

        try:
            run_halves()
            assert ws["acq"] == len(ws["plan"]) and ws["issued"] == len(ws["plan"]), (ws["acq"], ws["issued"], len(ws["plan"]))
        except _Stop:
            pass
        P.emit()
    return nc


def _consts():
    c = np.zeros((128, C_W), np.float32)
    c[:, C_ONES:C_ONES + 128] = 1.0
    s = np.arange(128)[:, None]
    t = np.arange(128)[None, :]
    c[:, C_TRI:C_TRI + 128] = np.where((s // 64 == t // 64) & (s > t), -1.0 / 16.0, 0.0)
    c[:, C_IND:C_IND + 2] = np.where(s // 64 == np.arange(2)[None, :], -1.0 / 16.0, 0.0)
    for g, w in enumerate(POOL_WINDOWS):
        c[:, C_INV + g * 16:C_INV + (g + 1) * 16] = 1.0 / np.minimum(np.arange(1, 17), w)
    c[:, C_ID:C_ID + 128] = np.eye(128, dtype=np.float32)
    return c


def _col(v):
    return np.ascontiguousarray(np.asarray(v, np.float32).reshape(16, 128).T)


def make_in_maps(x, norm_mix_g, w_in, pool_w, pool_scale, w_alpha, b_alpha, gla_norm_g,
                 w_gla_out, w_out, norm_mlp_g, w_mlp_up, w_mlp_down, norm_final_g, cores=range(8)):
    f = lambda a: np.ascontiguousarray(np.asarray(a, np.float32))
    vecs = np.concatenate([_col(norm_mix_g[0]), _col(pool_scale[0]), _col(np.asarray(gla_norm_g[0]).reshape(-1)),
                           _col(norm_mlp_g[0]), _col(norm_final_g)], axis=1)
    shared = {
        "w_in": f(w_in[0]),
        "pool_w": f(np.asarray(pool_w[0]).reshape(1024, 512)),
        "w_al": f(np.concatenate([np.asarray(w_alpha[0]), np.asarray(b_alpha[0])[None, :]], axis=0)),
        "w_gla": f(w_gla_out[0]),
        "w_out": f(w_out[0]),
        "w_up": f(w_mlp_up[0]),
        "w_dn": f(w_mlp_down[0]),
        "vecs": f(vecs),
        "cst": _consts(),
    }
    x = np.asarray(x, np.float32)
    maps = []
    for b in cores:
        m = dict(shared)
        m["xT"] = np.ascontiguousarray(x[b].T)
        maps.append(m)
    return maps


def kernel(**inputs):
    nc = build_program()
    in_maps = make_in_maps(**inputs)
    res = run_bass_kernel_spmd(nc, in_maps, core_ids=list(range(8)))
    out = np.stack([np.ascontiguousarray(np.asarray(r["outT"]).T) for r in res.results], axis=0)
    return out.astype(np.float32)
```

```python
import contextlib
import numpy as np
import concourse.bass as bass
import concourse.mybir as mybir
from concourse.bass_utils import run_bass_kernel_spmd

F32 = mybir.dt.float32
BF16 = mybir.dt.bfloat16
F32R = mybir.dt.float32r
AF = mybir.ActivationFunctionType
ALU = mybir.AluOpType

ENGS = ("pe", "act", "dve", "pool", "sp")

D = 2048
S_LEN = 2048
TH = 1024
NH = S_LEN // TH
KC = 16
EPS = 1e-6
IN_W = 11280
U0, Q0, K0, V0, G0, AL0, GP0, GG0 = 0, 1024, 2048, 3072, 5120, 7168, 7184, 9232
DFF = 8192
POOL_WINDOWS = (2, 4, 8, 16)

VG1, VPS, VGN, VG2, VGF = 0, 16, 32, 48, 64
C_ONES, C_TRI, C_IND, C_INV, C_ID = 0, 128, 256, 258, 322
C_W = 450


class Res:
    __slots__ = ("name", "last_w", "readers")

    def __init__(self, name):
        self.name = name
        self.last_w = None
        self.readers = {}


def _add_reader(res, ev):
    key = (ev[0], ev[1])
    if ev[2] > res.readers.get(key, -1):
        res.readers[key] = ev[2]


class Op:
    __slots__ = ("eng", "fn", "waits", "signal", "dma_sem", "dma_cnt")


class Prog:
    def __init__(self, nc):
        self.nc = nc
        self.ops = {e: [] for e in ENGS}
        self.dma_counts = {}
        self.n_dma_sems = 0
        self.out_events = []

    def new_dma_sem(self):
        i = self.n_dma_sems
        self.n_dma_sems += 1
        self.dma_counts[i] = 0
        return i

    @staticmethod
    def _deps(reads, writes):
        deps = {}

        def add(ev):
            key = (ev[0], ev[1])
            if ev[2] > deps.get(key, -1):
                deps[key] = ev[2]

        for r in reads:
            if r.last_w is not None:
                add(r.last_w)
        for w in writes:
            if w.last_w is not None:
                add(w.last_w)
            for key, v in w.readers.items():
                add((key[0], key[1], v))
        return deps

    def op(self, eng, fn, reads=(), writes=(), pe_acc=False):
        o = Op()
        o.eng = eng
        o.fn = fn
        o.signal = False
        o.dma_sem = None
        o.dma_cnt = 0
        idx = len(self.ops[eng])
        deps = self._deps(reads, writes)
        if pe_acc and eng == "pe":
            deps.pop(("eng", "pe"), None)
        o.waits = deps
        self.ops[eng].append(o)
        ev = ("eng", eng, idx)
        for r in reads:
            _add_reader(r, ev)
        for w in writes:
            w.last_w = ev
            w.readers = {}
        return ev

    def dma(self, queue, fn, sem, reads=(), writes=(), is_output=False):
        o = Op()
        o.eng = queue
        o.fn = fn
        o.signal = False
        o.waits = self._deps(reads, writes)
        self.dma_counts[sem] += 16
        o.dma_sem = sem
        o.dma_cnt = self.dma_counts[sem]
        self.ops[queue].append(o)
        ev = ("dma", sem, o.dma_cnt)
        for r in reads:
            _add_reader(r, ev)
        for w in writes:
            w.last_w = ev
            w.readers = {}
        if is_output:
            self.out_events.append(ev)
        return ev

    def emit(self, final_eng="sp"):
        nc = self.nc
        fin = Op()
        fin.eng = final_eng
        fin.fn = None
        fin.signal = False
        fin.dma_sem = None
        fin.dma_cnt = 0
        fw = {}
        for ev in self.out_events:
            key = (ev[0], ev[1])
            fw[key] = max(fw.get(key, -1), ev[2])
        fin.waits = fw
        self.ops[final_eng].append(fin)

        for e in ENGS:
            for o in self.ops[e]:
                for key, v in o.waits.items():
                    if key[0] == "eng":
                        self.ops[key[1]][v].signal = True
        rank = {}
        for e in ENGS:
            k = 0
            for i, o in enumerate(self.ops[e]):
                if o.signal:
                    k += 1
                    rank[(e, i)] = k

        with contextlib.ExitStack() as st:
            esem = {e: st.enter_context(nc.semaphore("es_" + e)) for e in ENGS}
            dsem = [st.enter_context(nc.semaphore("ds_%d" % i)) for i in range(self.n_dma_sems)]
            block = st.enter_context(nc.Block())

            def run_engine(e, handle):
                seen = {}
                for o in self.ops[e]:
                    for key, v in o.waits.items():
                        if key[0] == "eng":
                            c = rank[(key[1], v)]
                            sem = esem[key[1]]
                        else:
                            c = v
                            sem = dsem[key[1]]
                        if c > seen.get(key, 0):
                            handle.wait_ge(sem, c)
                            seen[key] = c
                    if o.fn is None:
                        continue
                    ins = None
                    for (meth, a, kw) in o.fn:
                        ins = getattr(handle, meth)(*a, **kw)
                    if o.dma_sem is not None:
                        ins.then_inc(dsem[o.dma_sem], 16)
                    elif o.signal:
                        ins.then_inc(esem[e], 1)

            @block.tensor
            def _(h):
                run_engine("pe", h)

            @block.scalar
            def _(h):
                run_engine("act", h)

            @block.vector
            def _(h):
                run_engine("dve", h)

            @block.gpsimd
            def _(h):
                run_engine("pool", h)

            @block.sync
            def _(h):
                run_engine("sp", h)


def I(meth, *a, **kw):
    return (meth, a, kw)


class _Stop(Exception):
    pass


def build_program(dbg=None, stop=0):
    nc = bass.Bass("TRN2", target_bir_lowering=False)
    xT = nc.dram_tensor("xT", [D, S_LEN], F32, kind="ExternalInput").ap()
    w_in = nc.dram_tensor("w_in", [D, IN_W], F32, kind="ExternalInput").ap()
    pool_w = nc.dram_tensor("pool_w", [1024, 512], F32, kind="ExternalInput").ap()
    w_al = nc.dram_tensor("w_al", [17, 1024], F32, kind="ExternalInput").ap()
    w_gla = nc.dram_tensor("w_gla", [D, D], F32, kind="ExternalInput").ap()
    w_out = nc.dram_tensor("w_out", [D, D], F32, kind="ExternalInput").ap()
    w_up = nc.dram_tensor("w_up", [D, DFF], F32, kind="ExternalInput").ap()
    w_dn = nc.dram_tensor("w_dn", [DFF, D], F32, kind="ExternalInput").ap()
    vecs_d = nc.dram_tensor("vecs", [128, 80], F32, kind="ExternalInput").ap()
    cst_d = nc.dram_tensor("cst", [128, C_W], F32, kind="ExternalInput").ap()
    outT = nc.dram_tensor("outT", [D, S_LEN], F32, kind="ExternalOutput").ap()
    dbg_t = {}
    if dbg:
        for name, shape in dbg.items():
            dbg_t[name] = nc.dram_tensor("dbg_" + name, list(shape), F32, kind="ExternalOutput").ap()

    P = Prog(nc)
    with contextlib.ExitStack() as st:
        base = (int(nc.sbuf_base) + 63) // 64 * 64
        ARENA = 210432
        st.enter_context(nc.sbuf_tensor("arena", [128, ARENA + 64], mybir.dt.uint8))
        cnt = [0]

        def at(off, shape, dt, nm):
            cnt[0] += 1
            return nc.alloc_sbuf_tensor_at("%s_%d" % (nm, cnt[0]), list(shape), dt, offset=base + off)

        OFF_HT, OFF_W, OFF_A1, OFF_A2, OFF_B0, OFF_B1 = 0, 32768, 81920, 114688, 147456, 163840
        OFF_S, OFF_SB, OFF_RSTD, OFF_M = 180224, 196608, 200704, 204800

        hT = at(OFF_HT, [128, KC, TH], BF16, "hT")
        Wr = [at(OFF_W + 16384 * i, [128, 8192], BF16, "Wr") for i in range(3)]
        Sst = at(OFF_S, [128, 4, 2, 512], F32, "S")
        Sb = [at(OFF_SB + 2048 * i, [128, 2, 512], BF16, "Sb") for i in range(2)]
        rstd = at(OFF_RSTD, [128, TH], F32, "rstd")
        o = OFF_M
        vecs = at(o, [128, 80], F32, "vecs"); o += 320
        ones = at(o, [128, 128], F32R, "ones"); o += 512
        triM = at(o, [128, 128], F32R, "triM"); o += 512
        ind = at(o, [128, 2], F32R, "ind"); o += 32
        invc = at(o, [128, 4, 16], F32, "invc"); o += 256
        ident = at(o, [128, 128], BF16, "ident"); o += 256
        Wa = at(o, [128, KC, 16], BF16, "Wa"); o += 512
        cd = at(o, [128, 128], F32, "cd"); o += 512
        smalls = at(o, [128, 16], F32, "smalls"); o += 64
        uhist = at(o, [128, 8, 16], F32, "uhist"); o += 512
        tmp16 = at(o, [128, 16], F32, "tmp16"); o += 64
        poolw = at(o, [128, 2, 512], BF16, "poolw"); o += 2048
        assert o <= ARENA, o

        xs = [at(OFF_B0 + 4096 * i, [128, TH], F32, "xs") for i in range(3)]
        sq = [at(OFF_B1 + 4096 * i, [128, TH], F32R, "sq") for i in range(2)]
        rl = [at(OFF_B1 + 8192 + 2048 * i, [128, 512], F32, "rl") for i in range(2)]
        walb = at(OFF_B0 + 0, [17, 1024], BF16, "walb")
        alow = at(OFF_B0 + 2048, [17, TH], BF16, "alow")
        e_t = [at(OFF_B0 + 4096 + 1024 * i, [128, 256], F32, "e_t") for i in range(2)]
        sp_t = [at(OFF_B0 + 6144 + 1024 * i, [128, 256], F32R, "sp_t") for i in range(2)]
        w_t = [at(OFF_B0 + 8192 + 1024 * i, [128, 256], F32, "w_t") for i in range(2)]
        ofm = [at(OFF_B0 + 10240 + 1024 * i, [128, 512], BF16, "ofm") for i in range(2)]
        junk = at(OFF_B0 + 12288, [128, 512], BF16, "junk")
        qT2 = [at(OFF_A2 + 0, [128, 2, TH], BF16, "qTa"), at(OFF_A2 + 24576, [128, 2, TH], BF16, "qTb")]
        kdec2 = [at(OFF_A2 + 4096, [128, 8, 256], BF16, "kdeca"), at(OFF_A2 + 28672, [128, 8, 256], BF16, "kdecb")]
        vtm2 = [at(OFF_A2 + 8192, [128, 8, 512], BF16, "vtma"), at(OFF_B1 + 0, [128, 8, 512], BF16, "vtmb")]
        sgt2 = [at(OFF_A2 + 16384, [128, 8, 512], BF16, "sgta"), at(OFF_B1 + 8192, [128, 8, 512], BF16, "sgtb")]
        ofT = at(OFF_A1, [128, KC, TH], BF16, "ofT")
        u_ext = at(OFF_A2 + 0, [128, 16 + TH], F32, "u_ext")
        ta = at(OFF_A2 + 4160, [128, 16 + TH], F32, "ta")
        tb = at(OFF_A2 + 8320, [128, 16 + TH], F32, "tb")
        t1 = at(OFF_A2 + 12480, [128, 4, TH], F32, "t1")
        s2t = at(OFF_A2 + 28864, [128, 512], F32, "s2t")
        d_g = at(OFF_SB, [128, 2, TH], BF16, "d_g")
        sqo = at(OFF_SB, [128, TH], F32R, "sqo")
        mixT = at(OFF_B0, [128, KC, TH], BF16, "mixT")
        x1T = at(OFF_A1, [128, KC, TH], F32, "x1T")
        hid = [at(OFF_B0 + 8192 * i, [128, 4, TH], BF16, "hid") for i in range(2)]
        orng = [at(OFF_B0 + 4096 * i, [128, TH], F32, "orng") for i in range(4)]

        ps = [st.enter_context(nc.psum_tensor("ps%d" % i, [128, 512], F32)) for i in range(7)]
        psT = st.enter_context(nc.psum_tensor("psT", [128, 1024], BF16))
        r_ps = [Res("ps%d" % i) for i in range(7)]
        r_psT = Res("psT")
        bank_ctr = [0]

        nb_excl = set()

        def nb():
            while True:
                b = bank_ctr[0] % 7
                bank_ctr[0] += 1
                if b not in nb_excl:
                    return b

        r_hT = [Res("hT%d" % k) for k in range(KC)]
        r_W = [Res("W%d" % i) for i in range(3)]
        r_S = [[Res("S%d%d" % (h, j)) for j in range(2)] for h in range(4)]
        r_rstd = Res("rstd")
        r_vecs, r_ones, r_tri, r_ind, r_invc, r_ident, r_Wa = (Res(n) for n in ("vecs", "ones", "tri", "ind", "invc", "ident", "Wa"))
        r_cd = [Res("cd%d" % h) for h in range(4)]
        r_small = Res("smalls")
        r_uhist = [Res("uh%d" % i) for i in range(8)]
        r_tmp16 = Res("tmp16")
        r_poolw = Res("poolw")

        occ = {"A1": [], "A2": [], "B0": [], "B1": [], "SB": []}

        def enter(regions, new):
            seed = {}
            for rg in regions:
                for r in occ[rg]:
                    if r.last_w is not None:
                        k = (r.last_w[0], r.last_w[1])
                        seed[k] = max(seed.get(k, -1), r.last_w[2])
                    for k, v in r.readers.items():
                        seed[k] = max(seed.get(k, -1), v)
            for r in new:
                r.readers = dict(seed)
            for rg in regions:
                occ[rg] = list(new)

        wsem = [P.new_dma_sem() for _ in range(3)]
        ws = {"plan": [], "issued": 0, "released": 0, "acq": 0}

        def Wv(s, ncols=512):
            return Wr[s][:, 0:KC * ncols].rearrange("p (k c) -> p k c", c=ncols)

        def ws_issue():
            while ws["issued"] < len(ws["plan"]) and ws["issued"] < ws["released"] + 3:
                n = ws["issued"]
                s = n % 3
                for f in ws["plan"][n]:
                    P.dma("pool", [f(s)], wsem[s], writes=[r_W[s]])
                ws["issued"] += 1

        def ws_acquire():
            n = ws["acq"]
            ws["acq"] += 1
            assert n < ws["issued"], "weight load not issued"
            return n % 3

        def ws_release():
            ws["released"] += 1
            ws_issue()

        def ld_cols(src, c0, ncols=512):
            def f(s):
                return I("dma_start", out=Wv(s, ncols), in_=src[:, c0:c0 + ncols].rearrange("(k p) c -> p k c", p=128))
            return f

        def ld_qk(hh):
            def fq(s):
                return I("dma_start", out=Wv(s)[:, :, 0:256],
                         in_=w_in[:, Q0 + hh * 256:Q0 + hh * 256 + 256].rearrange("(k p) c -> p k c", p=128))

            def fk(s):
                return I("dma_start", out=Wv(s)[:, :, 256:512],
                         in_=w_in[:, K0 + hh * 256:K0 + hh * 256 + 256].rearrange("(k p) c -> p k c", p=128))
            return [fq, fk]

        def ld_dn(fg):
            def f(s):
                return I("dma_start", out=Wr[s][:, :].rearrange("p (f c) -> p f c", c=2048),
                         in_=w_dn[fg * 512:(fg + 1) * 512, :].rearrange("(f p) c -> p f c", p=128))
            return f

        for hf in range(NH):
            for hh in range(4):
                ws["plan"].append(ld_qk(hh))
                ws["plan"].append([ld_cols(w_in, V0 + hh * 512)])
                ws["plan"].append([ld_cols(w_in, G0 + hh * 512)])
            for cg in range(4):
                ws["plan"].append([ld_cols(w_in, U0 + cg * 256, 256)])
                ws["plan"].append([ld_cols(w_in, GP0 + cg * 512)])
                ws["plan"].append([ld_cols(w_gla, cg * 512)])
                ws["plan"].append([ld_cols(w_in, GG0 + cg * 512)])
            for cg in range(4):
                ws["plan"].append([ld_cols(w_out, cg * 512)])
            ws["plan"].append([ld_cols(w_up, 0)])
            for fg in range(16):
                if fg + 1 < 16:
                    ws["plan"].append([ld_cols(w_up, (fg + 1) * 512)])
                ws["plan"].append([ld_dn(fg)])

        def const_load(q, dst, src, res):
            P.dma(q, [I("dma_start", out=dst, in_=src)], P.new_dma_sem(), writes=[res])

        const_load("sp", vecs[:, :], vecs_d[:, :], r_vecs)
        stg = at(OFF_RSTD, [128, 258], F32, "cstg")
        const_load("sp", stg[:, :], cst_d[:, C_ONES:C_ONES + 258], r_rstd)
        P.op("dve", [I("tensor_copy", out=ones[:, :], in_=stg[:, 0:128])], reads=[r_rstd], writes=[r_ones])
        P.op("dve", [I("tensor_copy", out=triM[:, :], in_=stg[:, 128:256])], reads=[r_rstd], writes=[r_tri])
        P.op("dve", [I("tensor_copy", out=ind[:, :], in_=stg[:, 256:258])], reads=[r_rstd], writes=[r_ind])
        const_load("sp", invc[:, :, :], cst_d[:, C_INV:C_INV + 64].rearrange("p (g t) -> p g t", t=16), r_invc)
        const_load("pool", ident[:, :], cst_d[:, C_ID:C_ID + 128], r_ident)
        const_load("pool", Wa[:, :, :], w_in[:, AL0:AL0 + 16].rearrange("(k p) c -> p k c", p=128), r_Wa)
        P.op("dve", [I("memset", Sst[:, :, :, :], 0.0)], writes=[r for hh in r_S for r in hh])
        P.op("dve", [I("memset", uhist[:, :, :], 0.0)], writes=r_uhist)

        s_x = [P.new_dma_sem() for _ in range(3)]
        s_x1 = [P.new_dma_sem() for _ in range(KC)]
        s_out = [P.new_dma_sem() for _ in range(4)]
        s_pw = P.new_dma_sem()
        s_wal = P.new_dma_sem()
        s_dbg = P.new_dma_sem()

        def dbg_dump(name, src_ap, reads):
            if name in dbg_t:
                P.dma("pool", [I("dma_start", out=dbg_t[name], in_=src_ap)], P.new_dma_sem(), reads=reads, is_output=True)

        def rms_rstd(src_list, src_res_list, r_sq, act_only=False):
            b0, b1 = nb(), nb()
            for k in range(KC):
                sl = k % 2
                if act_only or k % 2 == 0:
                    P.op("act", [I("activation", out=sq[sl][:, :], in_=src_list[k], func=AF.Square)],
                         reads=src_res_list[k], writes=[r_sq[sl]])
                else:
                    P.op("dve", [I("tensor_tensor", out=sq[sl][:, :], in0=src_list[k], in1=src_list[k], op=ALU.mult)],
                         reads=src_res_list[k], writes=[r_sq[sl]])
                P.op("pe", [I("matmul", ps[b][:, :], ones[:, :], sq[sl][:, j * 512:(j + 1) * 512], start=(k == 0), stop=(k == KC - 1))
                            for j, b in enumerate((b0, b1))],
                     reads=[r_sq[sl], r_ones], writes=[r_ps[b0], r_ps[b1]], pe_acc=True)
            for j, b in enumerate((b0, b1)):
                P.op("act", [I("activation", out=rstd[:, j * 512:(j + 1) * 512], in_=ps[b][:, :], func=AF.Ln, bias=EPS, scale=1.0 / D)],
                     reads=[r_ps[b]], writes=[r_rstd])
            P.op("act", [I("activation", out=rstd[:, :], in_=rstd[:, :], func=AF.Exp, scale=-0.5)], reads=[r_rstd], writes=[r_rstd])

        def proj_fm(s, m, ncols, rhs, rhs_res, nk=KC):
            b0, b1 = nb(), nb()
            wv = Wv(s, ncols)
            P.op("pe", [I("matmul", ps[b][:, :], wv[:, k, m * 128:(m + 1) * 128], rhs[:, k, j * 512:(j + 1) * 512],
                          start=(k == 0), stop=(k == nk - 1)) for k in range(nk) for j, b in enumerate((b0, b1))],
                 reads=[r_W[s]] + rhs_res, writes=[r_ps[b0], r_ps[b1]], pe_acc=True)
            return b0, b1

        pre_xa = []

        def run_halves():
          for hf in range(NH):
            t0 = hf * TH
            r_sq = [Res("sq%d" % i) for i in range(2)]
            enter(["B1"], r_sq)
            if len(pre_xa) == KC:
                r_xa = list(pre_xa)
                del pre_xa[:]
                occ["A1"] = [r[0] for r in r_xa]
                occ["A2"] = [r[0] for r in r_xa]
            else:
                r_xa = [[Res("xa%d" % k)] for k in range(KC)]
                enter(["A1", "A2"], [r[0] for r in r_xa])
                for k in range(KC):
                    P.dma("sp" if k % 2 == 0 else "act", [I("dma_start", out=x1T[:, k, :], in_=xT[k * 128:(k + 1) * 128, t0:t0 + TH])],
                          s_x1[k], writes=r_xa[k])
            if hf == 0:
                for i_ in range(3):
                    r_W[i_].readers = dict(r_xa[KC - 1 - i_][0].last_w and {(r_xa[KC - 1 - i_][0].last_w[0], r_xa[KC - 1 - i_][0].last_w[1]): r_xa[KC - 1 - i_][0].last_w[2]})
                ws_issue()
            rms_rstd([x1T[:, k, :] for k in range(KC)], r_xa, r_sq, act_only=(hf > 0))
            for k in range(KC):
                P.op("dve", [I("scalar_tensor_tensor", out=hT[:, k, :], in0=x1T[:, k, :], scalar=vecs[:, VG1 + k:VG1 + k + 1],
                               in1=rstd[:, :], op0=ALU.mult, op1=ALU.mult)],
                     reads=r_xa[k] + [r_vecs, r_rstd], writes=[r_hT[k]])
            if hf == 0:
                dbg_dump("hT", hT[:, :, :], r_hT)

            if stop == 1:
                raise _Stop()
            r_walb, r_alow = Res("walb"), Res("alow")
            r_e = [Res("e0"), Res("e1")]
            r_sp = [Res("sp0"), Res("sp1")]
            r_w = [Res("w0"), Res("w1")]
            r_ofm = [Res("ofm0"), Res("ofm1")]
            enter(["B0"], [r_walb, r_alow] + r_e + r_sp + r_w + r_ofm)
            r_qT = [[Res("qT%d_%d" % (p_, m)) for m in range(2)] for p_ in range(2)]
            r_kd = [[Res("kd%d_%d" % (p_, i)) for i in range(8)] for p_ in range(2)]
            r_v = [[Res("v%d_%d" % (p_, i)) for i in range(8)] for p_ in range(2)]
            r_sg = [[Res("sg%d_%d" % (p_, i)) for i in range(8)] for p_ in range(2)]
            enter(["A2"], r_qT[0] + r_qT[1] + r_kd[0] + r_kd[1] + r_v[0] + r_sg[0])
            enter(["B1"], r_v[1] + r_sg[1])
            r_ofT = [Res("ofT%d" % i) for i in range(KC)]
            enter(["A1"], r_ofT)
            r_Sb = [Res("Sb0"), Res("Sb1")]
            enter(["SB"], r_Sb)

            P.dma("pool", [I("dma_start", out=walb[:, :], in_=w_al[:, :])], s_wal, writes=[r_walb])
            P.op("dve", [I("memset", alow[:, :], 1.0)], writes=[r_alow])
            b0, b1 = nb(), nb()
            for k in range(KC):
                P.op("pe", [I("matmul", ps[b][0:16, :], Wa[:, k, :], hT[:, k, j * 512:(j + 1) * 512], start=(k == 0), stop=(k == KC - 1))
                            for j, b in enumerate((b0, b1))],
                     reads=[r_Wa, r_hT[k]], writes=[r_ps[b0], r_ps[b1]], pe_acc=True)
            for j, b in enumerate((b0, b1)):
                P.op("act", [I("activation", out=alow[0:16, j * 512:(j + 1) * 512], in_=ps[b][0:16, :], func=AF.Copy)],
                     reads=[r_ps[b]], writes=[r_alow])

            pool_ctr = [0]

            def nbp():
                b = 3 + pool_ctr[0] % 4
                pool_ctr[0] += 1
                return b

            def make_prep(hh):
                pb = hh % 2
                stt = {}

                def Z(tt):
                    i2 = tt % 2
                    bz = nbp()
                    P.op("pe", [I("matmul", ps[bz][:, 0:256], alow[0:17, tt * 128:(tt + 1) * 128], walb[0:17, hh * 256:(hh + 1) * 256],
                                  start=True, stop=True)],
                         reads=[r_alow, r_walb], writes=[r_ps[bz]], pe_acc=True)
                    P.op("act", [I("activation", out=e_t[i2][:, :], in_=ps[bz][:, 0:256], func=AF.Exp, scale=-1.0)],
                         reads=[r_ps[bz]], writes=[r_e[i2]])
                    P.op("act", [I("activation", out=sp_t[i2][:, :], in_=e_t[i2][:, :], func=AF.Ln, bias=1.0)],
                         reads=[r_e[i2]], writes=[r_sp[i2]])

                def pq(m):
                    def f():
                        if m == 0:
                            stt["qk"] = ws_acquire()
                        s = stt["qk"]
                        ba, bb = nbp(), nbp()
                        wv = Wv(s)
                        P.op("pe", [I("matmul", ps[b][:, :], wv[:, k, m * 128:(m + 1) * 128], hT[:, k, j * 512:(j + 1) * 512],
                                      start=(k == 0), stop=(k == KC - 1)) for k in range(KC) for j, b in enumerate((ba, bb))],
                             reads=[r_W[s]] + r_hT, writes=[r_ps[ba], r_ps[bb]], pe_acc=True)
                        for j, b in enumerate((ba, bb)):
                            P.op("act", [I("activation", out=qT2[pb][:, m, j * 512:(j + 1) * 512], in_=ps[b][:, :], func=AF.Copy, scale=1.0 / 16.0)],
                                 reads=[r_ps[b]], writes=[r_qT[pb][m]])
                    return f

                def pk(tt):
                    def f():
                        s = stt["qk"]
                        wqk = Wv(s)
                        i2 = tt % 2
                        tsl = slice(tt * 128, (tt + 1) * 128)
                        if tt == 0:
                            Z(0)
                        bk = nbp()
                        P.op("pe", [I("matmul", ps[bk][:, 0:256], hT[:, k, tsl], wqk[:, k, 256:512], start=(k == 0), stop=(k == KC - 1))
                                    for k in range(KC)],
                             reads=[r_W[s]] + r_hT, writes=[r_ps[bk]], pe_acc=True)
                        if tt + 1 < 8:
                            Z(tt + 1)
                        brv = nbp()
                        P.op("pe", [I("matmul", ps[brv][:, 0:256], triM[:, :], sp_t[i2][:, :], start=True, stop=True)]
                                   + [I("matmul", ps[brv][:, 256 + 2 * j:256 + 2 * j + 2], sp_t[i2][:, j * 128:(j + 1) * 128], ind[:, :],
                                        start=True, stop=True) for j in range(2)],
                             reads=[r_tri, r_ind, r_sp[i2]], writes=[r_ps[brv]], pe_acc=True)
                        P.op("act", [I("activation", out=cd[:, (hh * 2 + j) * 16 + 2 * tt:(hh * 2 + j) * 16 + 2 * tt + 2],
                                       in_=ps[brv][:, 256 + 2 * j:256 + 2 * j + 2], func=AF.Exp) for j in range(2)]
                                    + [I("activation", out=w_t[i2][:, :], in_=ps[brv][:, 0:256], func=AF.Exp)],
                             reads=[r_ps[brv]], writes=[r_cd[hh], r_w[i2]])
                        P.op("dve", [I("tensor_tensor", out=kdec2[pb][:, tt, :], in0=ps[bk][:, 0:256], in1=w_t[i2][:, :], op=ALU.mult)],
                             reads=[r_ps[bk], r_w[i2]], writes=[r_kd[pb][tt]])
                        if tt == 7:
                            ws_release()
                    return f

                def pv(tt):
                    def f():
                        if tt == 0:
                            stt["v"] = ws_acquire()
                        s = stt["v"]
                        wvv = Wv(s)
                        bv = nbp()
                        P.op("pe", [I("matmul", ps[bv][:, :], hT[:, k, tt * 128:(tt + 1) * 128], wvv[:, k, :], start=(k == 0), stop=(k == KC - 1))
                                    for k in range(KC)],
                             reads=[r_W[s]] + r_hT, writes=[r_ps[bv]], pe_acc=True)
                        P.op("dve", [I("tensor_copy", out=vtm2[pb][:, tt, :], in_=ps[bv][:, :])], reads=[r_ps[bv]], writes=[r_v[pb][tt]])
                        if tt == 7:
                            ws_release()
                    return f

                def pg(tt):
                    def f():
                        if tt == 0:
                            stt["g"] = ws_acquire()
                        s = stt["g"]
                        wgg = Wv(s)
                        bg = nbp()
                        P.op("pe", [I("matmul", ps[bg][:, :], hT[:, k, tt * 128:(tt + 1) * 128], wgg[:, k, :], start=(k == 0), stop=(k == KC - 1))
                                    for k in range(KC)],
                             reads=[r_W[s]] + r_hT, writes=[r_ps[bg]], pe_acc=True)
                        P.op("act", [I("activation", out=sgt2[pb][:, tt, :], in_=ps[bg][:, :], func=AF.Silu)],
                             reads=[r_ps[bg]], writes=[r_sg[pb][tt]])
                        if tt == 7:
                            ws_release()
                    return f

                return [pq(0), pq(1)] + [pk(tt) for tt in range(8)] + [pv(tt) for tt in range(8)] + [pg(tt) for tt in range(8)]

            r_sm2 = [Res("smalls_l0"), Res("smalls_l1")]

            def kv_mm(hh, c, lane, j):
                pb = hh % 2
                tt, ph = c // 2, c % 2
                prt = slice(ph * 64, (ph + 1) * 64)
                kb = (0, 1) if not lane else (3, 4)
                return (I("matmul", ps[kb[j]][:, :], kdec2[pb][prt, tt, j * 128:(j + 1) * 128], vtm2[pb][prt, tt, :], start=True, stop=True),
                        [r_kd[pb][tt], r_v[pb][tt]], [r_ps[kb[j]]])

            def o_mm(hh, c, lane, j):
                pb = hh % 2
                ph = c % 2
                prt = slice(ph * 64, (ph + 1) * 64)
                rb = (c % 2) if lane is None else lane
                bO = 2 if not lane else 5
                return (I("matmul", ps[bO][prt, :], qT2[pb][:, j, c * 64:(c + 1) * 64], Sb[rb][:, j, :], start=(j == 0), stop=(j == 1)),
                        [r_qT[pb][0], r_qT[pb][1], r_Sb[rb]], [r_ps[bO]])

            def joint_pe(items, fn):
                mm, rd, wr = [], [], []
                for j in range(2):
                    for ln, hh, c in items:
                        i_, r_, w_ = fn(hh, c, ln, j)
                        mm.append(i_)
                        rd += r_
                        wr += w_
                if mm:
                    P.op("pe", mm, reads=rd, writes=wr, pe_acc=True)

            def rec_kv(hh, c, lane=None, skip_pe=False):
                pb = hh % 2
                tt, ph = c // 2, c % 2
                prt = slice(ph * 64, (ph + 1) * 64)
                kb = (0, 1) if not lane else (3, 4)
                if not skip_pe:
                    P.op("pe", [I("matmul", ps[kb[j]][:, :], kdec2[pb][prt, tt, j * 128:(j + 1) * 128], vtm2[pb][prt, tt, :], start=True, stop=True)
                                for j in range(2)],
                         reads=[r_kd[pb][tt], r_v[pb][tt]], writes=[r_ps[kb[0]], r_ps[kb[1]]], pe_acc=True)
                rb = (c % 2) if lane is None else lane
                for j in range(2):
                    ci = (hh * 2 + j) * 16 + c
                    P.op("dve", [I("scalar_tensor_tensor", out=Sst[:, hh, j, :], in0=Sst[:, hh, j, :], scalar=cd[:, ci:ci + 1],
                                   in1=ps[kb[j]][:, :], op0=ALU.mult, op1=ALU.add)],
                         reads=[r_S[hh][j], r_cd[hh], r_ps[kb[j]]], writes=[r_S[hh][j]])
                if lane is None:
                    rec_cast(hh, c, lane)

            def rec_cast(hh, c, lane=None):
                rb = (c % 2) if lane is None else lane
                P.op("act", [I("activation", out=Sb[rb][:, :, :], in_=Sst[:, hh, :, :], func=AF.Copy)],
                     reads=[r_S[hh][0], r_S[hh][1]], writes=[r_Sb[rb]])

            def rec_o(hh, c, lane=None, skip_pe=False):
                pb = hh % 2
                tt, ph = c // 2, c % 2
                prt = slice(ph * 64, (ph + 1) * 64)
                rb = (c % 2) if lane is None else lane
                bO = 2 if not lane else 5
                if not skip_pe:
                    P.op("pe", [I("matmul", ps[bO][prt, :], qT2[pb][:, j, c * 64:(c + 1) * 64], Sb[rb][:, j, :], start=(j == 0), stop=(j == 1))
                                for j in range(2)],
                         reads=[r_qT[pb][0], r_qT[pb][1], r_Sb[rb]], writes=[r_ps[bO]], pe_acc=True)
                if ph == 1:
                    i2 = (tt % 2) if lane is None else lane
                    sc = 3 * i2
                    rs_ = r_sm2[0] if not lane else r_sm2[1]
                    P.op("act", [I("activation", out=junk[:, :], in_=ps[bO][:, :], func=AF.Square, accum_out=smalls[:, sc:sc + 1])],
                         reads=[r_ps[bO]], writes=[rs_])
                    P.op("act", [I("activation", out=smalls[:, sc + 1:sc + 2], in_=smalls[:, sc:sc + 1], func=AF.Sqrt, bias=EPS, scale=1.0 / 512.0)],
                         reads=[rs_], writes=[rs_])
                    P.op("dve", [I("reciprocal", out=smalls[:, sc + 2:sc + 3], in_=smalls[:, sc + 1:sc + 2])],
                         reads=[rs_], writes=[rs_])
                    P.op("dve", [I("scalar_tensor_tensor", out=ofm[i2][:, :], in0=ps[bO][:, :], scalar=smalls[:, sc + 2:sc + 3],
                                   in1=sgt2[pb][:, tt, :], op0=ALU.mult, op1=ALU.mult)],
                         reads=[r_ps[bO], rs_, r_sg[pb][tt]], writes=[r_ofm[i2]])

            def rec_tr(hh, c, lane=None):
                tt = c // 2
                i2 = (tt % 2) if lane is None else lane
                P.op("pe", [I("transpose", psT[:, i * 128:(i + 1) * 128], ofm[i2][:, i * 128:(i + 1) * 128], ident[:, :]) for i in range(4)],
                     reads=[r_ofm[i2], r_ident], writes=[r_psT], pe_acc=True)
                P.op("act", [I("activation", out=ofT[:, hh * 4 + i, tt * 128:(tt + 1) * 128], in_=psT[:, i * 128:(i + 1) * 128],
                               func=AF.Copy, scale=vecs[:, VGN + hh * 4 + i:VGN + hh * 4 + i + 1]) for i in range(4)],
                     reads=[r_psT, r_vecs], writes=[r_ofT[hh * 4 + i] for i in range(4)])

            queue = list(make_prep(0))
            while queue:
                queue.pop(0)()
            if hf == 0:
                dbg_dump("qT", qT2[0][:, :, :], r_qT[0])
                dbg_dump("kdec", kdec2[0][:, :, :], r_kd[0])
                dbg_dump("vtm", vtm2[0][:, :, :], r_v[0])
                dbg_dump("sgt", sgt2[0][:, :, :], r_sg[0])
                dbg_dump("cd", cd[:, 0:32], [r_cd[0]])
            if stop == 2:
                raise _Stop()
            for hh in range(2):
                queue = list(make_prep(hh + 1))
                pend_tr = []
                for c in range(16):
                    rec_kv(hh, c)
                    if queue:
                        queue.pop(0)()
                    if pend_tr and pend_tr[0] <= c - 3:
                        rec_tr(hh, pend_tr.pop(0))
                    if c >= 1:
                        rec_o(hh, c - 1)
                        if (c - 1) % 2 == 1:
                            pend_tr.append(c - 1)
                    if queue:
                        queue.pop(0)()
                rec_o(hh, 15)
                pend_tr.append(15)
                while queue:
                    queue.pop(0)()
                for c_ in pend_tr:
                    rec_tr(hh, c_)
            queue = list(make_prep(3))
            while queue:
                queue.pop(0)()
            lanes = ((0, 2), (1, 3))
            pend2 = {0: [], 1: []}
            for s_ in range(18):
                joint_pe([(ln, hh, s_ - ln) for ln, hh in lanes if 0 <= s_ - ln < 16], kv_mm)
                for ln, hh in lanes:
                    c = s_ - ln
                    if 0 <= c < 16:
                        rec_kv(hh, c, lane=ln, skip_pe=True)
                for ln, hh in lanes:
                    c = s_ - ln
                    if pend2[ln] and pend2[ln][0] <= c - 2:
                        rec_tr(hh, pend2[ln].pop(0), lane=ln)
                joint_pe([(ln, hh, s_ - ln - 1) for ln, hh in lanes if 1 <= s_ - ln <= 16], o_mm)
                for ln, hh in lanes:
                    c = s_ - ln
                    if 1 <= c <= 16:
                        rec_o(hh, c - 1, lane=ln, skip_pe=True)
                        if (c - 1) % 2 == 1:
                            pend2[ln].append(c - 1)
                for ln, hh in lanes:
                    c = s_ - ln
                    if 0 <= c < 16:
                        rec_cast(hh, c, lane=ln)
            for ln, hh in lanes:
                for c_ in pend2[ln]:
                    rec_tr(hh, c_, lane=ln)
            if stop == 3:
                raise _Stop()
            if hf == 0:
                dbg_dump("ofT", ofT[:, :, :], r_ofT)

            r_u, r_ta, r_tb, r_s2 = Res("u_ext"), Res("ta"), Res("tb"), Res("s2t")
            r_t1 = [[Res("t1_%d%d" % (m, j)) for j in range(2)] for m in range(4)]
            enter(["A2"], [r_u, r_ta, r_tb, r_s2] + [r for mm_ in r_t1 for r in mm_])
            r_dg = [Res("dg0"), Res("dg1")]
            enter(["SB"], r_dg)
            r_mix = [Res("mix%d" % i) for i in range(KC)]
            enter(["B0", "B1"], r_mix)
            L = 16 + TH

            for cg in range(4):
                wdw = POOL_WINDOWS[cg]
                P.dma("pool", [I("dma_start", out=poolw[:, :, :], in_=pool_w[cg * 256:(cg + 1) * 256, :].rearrange("(k p) c -> p k c", p=128))],
                      s_pw, writes=[r_poolw])
                s_u = ws_acquire()
                for ct in range(2):
                    cti = cg * 2 + ct
                    bu = proj_fm(s_u, ct, 256, hT, r_hT)
                    P.op("dve", [I("tensor_copy", out=u_ext[:, 0:16], in_=uhist[:, cti, :])], reads=[r_uhist[cti]], writes=[r_u])
                    for j, b in enumerate(bu):
                        P.op("act", [I("activation", out=u_ext[:, 16 + j * 512:16 + (j + 1) * 512], in_=ps[b][:, :], func=AF.Copy)],
                             reads=[r_ps[b]], writes=[r_u])
                    P.op("dve", [I("tensor_copy", out=uhist[:, cti, :], in_=u_ext[:, TH:TH + 16])], reads=[r_u], writes=[r_uhist[cti]])
                    P.op("dve", [I("tensor_tensor", out=ta[:, 1:L], in0=u_ext[:, 1:L], in1=u_ext[:, 0:L - 1], op=ALU.add)],
                         reads=[r_u], writes=[r_ta])
                    cur, r_cur = ta, r_ta
                    if wdw >= 4:
                        P.op("dve", [I("tensor_tensor", out=tb[:, 3:L], in0=ta[:, 3:L], in1=ta[:, 1:L - 2], op=ALU.add)],
                             reads=[r_ta], writes=[r_tb])
                        cur, r_cur = tb, r_tb
                    if wdw >= 8:
                        P.op("dve", [I("tensor_tensor", out=ta[:, 7:L], in0=tb[:, 7:L], in1=tb[:, 3:L - 4], op=ALU.add)],
                             reads=[r_tb], writes=[r_ta])
                        cur, r_cur = ta, r_ta
                    if wdw >= 16:
                        P.op("dve", [I("tensor_tensor", out=tb[:, 15:L], in0=ta[:, 15:L], in1=ta[:, 7:L - 8], op=ALU.add)],
                             reads=[r_ta], writes=[r_tb])
                        cur, r_cur = tb, r_tb
                    P.op("dve", [I("scalar_tensor_tensor", out=d_g[:, ct, :], in0=cur[:, 16:L], scalar=1.0 / wdw, in1=u_ext[:, 16:L],
                                   op0=ALU.mult, op1=ALU.subtract)],
                         reads=[r_cur, r_u], writes=[r_dg[ct]])
                    if hf == 0:
                        P.op("dve", [I("tensor_tensor", out=tmp16[:, :], in0=cur[:, 16:32], in1=invc[:, cg, :], op=ALU.mult)],
                             reads=[r_cur, r_invc], writes=[r_tmp16])
                        P.op("dve", [I("tensor_tensor", out=d_g[:, ct, 0:16], in0=tmp16[:, :], in1=u_ext[:, 16:32], op=ALU.subtract)],
                             reads=[r_tmp16, r_u], writes=[r_dg[ct]])
                ws_release()
                if hf == 0:
                    dbg_dump("d_g%d" % cg, d_g[:, :, :], r_dg)
                s_gp = ws_acquire()
                for m in range(4):
                    bg_ = proj_fm(s_gp, m, 512, hT, r_hT)
                    for j in range(2):
                        P.op("act", [I("activation", out=t1[:, m, j * 512:(j + 1) * 512], in_=ps[bg_[j]][:, :], func=AF.Sigmoid)],
                             reads=[r_ps[bg_[j]]], writes=[r_t1[m][j]])
                ws_release()
                for m in range(4):
                    dt_ = cg * 4 + m
                    by = (nb(), nb())
                    P.op("pe", [I("matmul", ps[b][:, :], poolw[:, k, m * 128:(m + 1) * 128], d_g[:, k, j * 512:(j + 1) * 512],
                                  start=(k == 0), stop=(k == 1)) for k in range(2) for j, b in enumerate(by)],
                         reads=[r_poolw] + r_dg, writes=[r_ps[by[0]], r_ps[by[1]]], pe_acc=True)
                    for j in range(2):
                        P.op("dve", [I("scalar_tensor_tensor", out=t1[:, m, j * 512:(j + 1) * 512], in0=ps[by[j]][:, :],
                                       scalar=vecs[:, VPS + dt_:VPS + dt_ + 1], in1=t1[:, m, j * 512:(j + 1) * 512],
                                       op0=ALU.mult, op1=ALU.mult)],
                             reads=[r_ps[by[j]], r_vecs, r_t1[m][j]], writes=[r_t1[m][j]])
                s_wg = ws_acquire()
                s_gg = ws_acquire()
                for m in range(4):
                    dt_ = cg * 4 + m
                    byg = proj_fm(s_wg, m, 512, ofT, r_ofT)
                    bgg = proj_fm(s_gg, m, 512, hT, r_hT)
                    for j in range(2):
                        P.op("act", [I("activation", out=s2t[:, :], in_=ps[bgg[j]][:, :], func=AF.Sigmoid)],
                             reads=[r_ps[bgg[j]]], writes=[r_s2])
                        P.op("dve", [I("tensor_tensor", out=s2t[:, :], in0=ps[byg[j]][:, :], in1=s2t[:, :], op=ALU.mult)],
                             reads=[r_ps[byg[j]], r_s2], writes=[r_s2])
                        P.op("dve", [I("tensor_tensor", out=mixT[:, dt_, j * 512:(j + 1) * 512], in0=t1[:, m, j * 512:(j + 1) * 512],
                                       in1=s2t[:, :], op=ALU.add)],
                             reads=[r_t1[m][j], r_s2], writes=[r_mix[dt_]])
                ws_release(); ws_release()
            if hf == 0:
                dbg_dump("mixT", mixT[:, :, :], r_mix)

            if stop == 4:
                raise _Stop()
            r_x1 = [[Res("x1_%d_%d" % (m, j)) for j in range(2)] for m in range(KC)]
            enter(["A1", "A2"], [r for mm_ in r_x1 for r in mm_])
            for m in range(KC):
                P.dma("sp", [I("dma_start", out=x1T[:, m, :], in_=xT[m * 128:(m + 1) * 128, t0:t0 + TH])], s_x1[m], writes=r_x1[m])
            r_sqo = Res("sqo")
            enter(["SB"], [r_sqo])
            nb_excl.update((5, 6))

            def stats_chunks(ks):
                for k in ks:
                    P.op("act", [I("activation", out=sqo[:, :], in_=x1T[:, k, :], func=AF.Square)], reads=r_x1[k], writes=[r_sqo])
                    P.op("pe", [I("matmul", ps[b][:, :], ones[:, :], sqo[:, j * 512:(j + 1) * 512], start=(k == 0), stop=(k == KC - 1))
                                for j, b in enumerate((5, 6))],
                         reads=[r_sqo, r_ones], writes=[r_ps[5], r_ps[6]], pe_acc=True)

            def stats_finish():
                for j, b in enumerate((5, 6)):
                    P.op("act", [I("activation", out=rstd[:, j * 512:(j + 1) * 512], in_=ps[b][:, :], func=AF.Ln, bias=EPS, scale=1.0 / D)],
                         reads=[r_ps[b]], writes=[r_rstd])
                P.op("act", [I("activation", out=rstd[:, :], in_=rstd[:, :], func=AF.Exp, scale=-0.5)], reads=[r_rstd], writes=[r_rstd])
                nb_excl.clear()

            for cg in range(4):
                s_o = ws_acquire()
                for m in range(4):
                    dt_ = cg * 4 + m
                    bo = proj_fm(s_o, m, 512, mixT, r_mix)
                    for j in range(2):
                        P.op("dve", [I("tensor_tensor", out=x1T[:, dt_, j * 512:(j + 1) * 512], in0=x1T[:, dt_, j * 512:(j + 1) * 512],
                                       in1=ps[bo[j]][:, :], op=ALU.add)],
                             reads=[r_ps[bo[j]], r_x1[dt_][j]], writes=[r_x1[dt_][j]])
                ws_release()
                for k in range(cg * 4, cg * 4 + 4):
                    P.op("dve", [I("tensor_scalar", out=hT[:, k, :], in0=x1T[:, k, :], scalar1=vecs[:, VG2 + k:VG2 + k + 1], scalar2=None,
                                   op0=ALU.mult)],
                         reads=r_x1[k] + [r_vecs], writes=[r_hT[k]])
                if cg >= 1:
                    stats_chunks(range((cg - 1) * 4, cg * 4))
            if hf == 0:
                dbg_dump("x1T", x1T[:, :, :], [r for mm_ in r_x1 for r in mm_])

            if stop == 5:
                raise _Stop()
            r_sq = [Res("sq2_%d" % i) for i in range(2)]
            r_rl = [Res("rl%d" % i) for i in range(2)]
            enter(["B1"], r_sq + r_rl)
            r_hid = [[Res("hid%d_%d" % (i, m)) for m in range(4)] for i in range(2)]
            enter(["B0"], [r for hh_ in r_hid for r in hh_])
            if hf == 0:
                dbg_dump("h2T", hT[:, :, :], r_hT)

            def mlp_up(fg):
                s_up = ws_acquire()
                hb = fg % 2
                for m in range(4):
                    bu = proj_fm(s_up, m, 512, hT, r_hT)
                    if fg == 0 and m == 0:
                        stats_chunks(range(12, 16))
                        stats_finish()
                    for j in range(2):
                        ri = j
                        P.op("act", [I("activation", out=rl[ri][:, :], in_=ps[bu[j]][:, :], func=AF.Relu)],
                             reads=[r_ps[bu[j]]], writes=[r_rl[ri]])
                        P.op("dve", [I("tensor_tensor", out=rl[ri][:, :], in0=rl[ri][:, :], in1=rstd[:, j * 512:(j + 1) * 512], op=ALU.mult)],
                             reads=[r_rl[ri], r_rstd], writes=[r_rl[ri]])
                        P.op("dve", [I("tensor_tensor", out=hid[hb][:, m, j * 512:(j + 1) * 512], in0=rl[ri][:, :], in1=rl[ri][:, :], op=ALU.mult)],
                             reads=[r_rl[ri]], writes=[r_hid[hb][m]])
                ws_release()

            def mlp_dn(fg):
                s_dn = ws_acquire()
                hb = fg % 2
                wd = Wr[s_dn][:, :].rearrange("p (f c) -> p f c", c=2048)
                for m in range(KC):
                    bd = (nb(), nb())
                    P.op("pe", [I("matmul", ps[b][:, :], wd[:, f, m * 128:(m + 1) * 128], hid[hb][:, f, j * 512:(j + 1) * 512],
                                  start=(f == 0), stop=(f == 3)) for f in range(4) for j, b in enumerate(bd)],
                         reads=[r_W[s_dn]] + r_hid[hb], writes=[r_ps[bd[0]], r_ps[bd[1]]], pe_acc=True)
                    for j in range(2):
                        P.op("dve", [I("tensor_tensor", out=x1T[:, m, j * 512:(j + 1) * 512], in0=x1T[:, m, j * 512:(j + 1) * 512],
                                       in1=ps[bd[j]][:, :], op=ALU.add)],
                             reads=[r_ps[bd[j]], r_x1[m][j]], writes=[r_x1[m][j]])
                ws_release()

            mlp_up(0)
            for fg in range(16):
                if fg + 1 < 16:
                    mlp_up(fg + 1)
                mlp_dn(fg)

            if stop == 6:
                raise _Stop()
            r_or = [Res("or%d" % i) for i in range(4)]
            enter(["B0"], r_or)
            rms_rstd([x1T[:, k, :] for k in range(KC)], r_x1, r_sq, act_only=True)
            for k in range(KC):
                oi = k % 4
                P.op("dve", [I("scalar_tensor_tensor", out=orng[oi][:, :], in0=x1T[:, k, :], scalar=vecs[:, VGF + k:VGF + k + 1],
                               in1=rstd[:, :], op0=ALU.mult, op1=ALU.mult)],
                     reads=r_x1[k] + [r_vecs, r_rstd], writes=[r_or[oi]])
                P.dma("sp", [I("dma_start", out=outT[k * 128:(k + 1) * 128, t0:t0 + TH], in_=orng[oi][:, :])],
                      s_out[oi], reads=[r_or[oi]], is_output=True)
                if hf + 1 < NH:
                    nr = Res("xa%d" % k)
                    seed = {}
                    for r in r_x1[k]:
                        if r.last_w is not None:
                            kk = (r.last_w[0], r.last_w[1])
                            seed[kk] = max(seed.get(kk, -1), r.last_w[2])
                        for kk, v in r.readers.items():
                            seed[kk] = max(seed.get(kk, -1), v)
                    nr.readers = seed
                    P.dma("sp", [I("dma_start", out=x1T[:, k, :], in_=xT[k * 128:(k + 1) * 128, t0 + TH:t0 + 2 * TH])], s_x1[k], writes=[nr])
                    pre_xa.append([nr])

        try:
            run_halves()
            assert ws["acq"] == len(ws["plan"]) and ws["issued"] == len(ws["plan"]), (ws["acq"], ws["issued"], len(ws["plan"]))
        except _Stop:
            pass
        P.emit()
    return nc


def _consts():
    c = np.zeros((128, C_W), np.float32)
    c[:, C_ONES:C_ONES + 128] = 1.0
    s = np.arange(128)[:, None]
    t = np.arange(128)[None, :]
    c[:, C_TRI:C_TRI + 128] = np.where((s // 64 == t // 64) & (s > t), -1.0 / 16.0, 0.0)
    c[:, C_IND:C_IND + 2] = np.where(s // 64 == np.arange(2)[None, :], -1.0 / 16.0, 0.0)
    for g, w in enumerate(POOL_WINDOWS):
        c[:, C_INV + g * 16:C_INV + (g + 1) * 16] = 1.0 / np.minimum(np.arange(1, 17), w)
    c[:, C_ID:C_ID + 128] = np.eye(128, dtype=np.float32)
    return c


def _col(v):
    return np.ascontiguousarray(np.asarray(v, np.float32).reshape(16, 128).T)


def make_in_maps(x, norm_mix_g, w_in, pool_w, pool_scale, w_alpha, b_alpha, gla_norm_g,
                 w_gla_out, w_out, norm_mlp_g, w_mlp_up, w_mlp_down, norm_final_g, cores=range(8)):
    f = lambda a: np.ascontiguousarray(np.asarray(a, np.float32))
    vecs = np.concatenate([_col(norm_mix_g[0]), _col(pool_scale[0]), _col(np.asarray(gla_norm_g[0]).reshape(-1)),
                           _col(norm_mlp_g[0]), _col(norm_final_g)], axis=1)
    shared = {
        "w_in": f(w_in[0]),
        "pool_w": f(np.asarray(pool_w[0]).reshape(1024, 512)),
        "w_al": f(np.concatenate([np.asarray(w_alpha[0]), np.asarray(b_alpha[0])[None, :]], axis=0)),
        "w_gla": f(w_gla_out[0]),
        "w_out": f(w_out[0]),
        "w_up": f(w_mlp_up[0]),
        "w_dn": f(w_mlp_down[0]),
        "vecs": f(vecs),
        "cst": _consts(),
    }
    x = np.asarray(x, np.float32)
    maps = []
    for b in cores:
        m = dict(shared)
        m["xT"] = np.ascontiguousarray(x[b].T)
        maps.append(m)
    return maps


def kernel(**inputs):
    nc = build_program()
    in_maps = make_in_maps(**inputs)
    res = run_bass_kernel_spmd(nc, in_maps, core_ids=list(range(8)))
    out = np.stack([np.ascontiguousarray(np.asarray(r["outT"]).T) for r in res.results], axis=0)
    return out.astype(np.float32)
```
